# Optimizing a Trainium2 kernel written in Bass

```python
import jax, jax.numpy as jnp
from jax import lax
import numpy as np

D_MODEL = 2048
BATCH = 4
SEQ = 8192
DEPTH = 1
DEC_BATCH = 1
DEC_SEQ = 16384
PAST_LEN = 128

GRID_W = 64
NA_KH = 8
NA_KW = 16
H_A = 16
DH_A = 64
D_A = H_A * DH_A
H_B = 16
DH_B = 64
D_B = H_B * DH_B
CHUNK = 128
D_MIX = D_A + D_B
D_IN_MIX = 3 * D_A + 2 * D_B
D_FF = 5632
EPS = 1e-6

kernel_name = "hymba_natten_gmlp_macaron_encoder"


def rmsnorm(x, g):
    x32 = x.astype(jnp.float32)
    y = x32 * lax.rsqrt(jnp.mean(x32 * x32, axis=-1, keepdims=True) + EPS)
    return (y * g.astype(jnp.float32)).astype(x.dtype)


def swiglu(h, w_in, w_out):
    gate, up = jnp.split(h @ w_in, 2, axis=-1)
    return (jax.nn.silu(gate) * up) @ w_out


def neighbourhood_attention(q, k, v, rpb):
    b, t, h, dh = q.shape
    rows = t // GRID_W
    kh = min(NA_KH, rows)
    qg = q.reshape(b, rows, GRID_W, h, dh)
    kg = k.reshape(b, rows, GRID_W, h, dh)
    vg = v.reshape(b, rows, GRID_W, h, dh)
    r = jnp.arange(rows)
    row_start = jnp.clip(r - kh // 2, 0, rows - kh)
    key_rows = row_start[:, None] + jnp.arange(kh)[None, :]
    k_blk = jnp.take(kg, key_rows, axis=1)
    v_blk = jnp.take(vg, key_rows, axis=1)
    c = jnp.arange(GRID_W)
    col_start = jnp.clip(c - NA_KW // 2, 0, GRID_W - NA_KW)
    col_off = c[None, :] - c[:, None]
    col_valid = (c[None, :] >= col_start[:, None]) & (c[None, :] < col_start[:, None] + NA_KW)
    row_off = key_rows - r[:, None]
    ri = (row_off + NA_KH - 1)[:, None, :, None]
    ci = (jnp.clip(col_off, -(NA_KW - 1), NA_KW - 1) + NA_KW - 1)[None, :, None, :]
    bias = rpb[:, ri, ci].astype(jnp.float32)
    bias = jnp.where(col_valid[None, :, None, :], bias, -jnp.inf)
    scale = dh ** -0.5
    s = jnp.einsum('brqhd,brkwhd->bhrqkw', qg, k_blk).astype(jnp.float32) * scale + bias
    p = jax.nn.softmax(s.reshape(b, h, rows, GRID_W, kh * GRID_W), axis=-1)
    p = p.reshape(b, h, rows, GRID_W, kh, GRID_W).astype(v.dtype)
    o = jnp.einsum('bhrqkw,brkwhd->brqhd', p, v_blk)
    return o.reshape(b, t, h * dh)


def spatial_gating(u, g, gate_norm, w_s, b_s):
    b, t, _ = u.shape
    g = rmsnorm(g, gate_norm)
    gc = g.reshape(b, t // CHUNK, CHUNK, H_B, DH_B)
    mixed = jnp.einsum('hpq,bcqhd->bcphd', w_s, gc) + b_s.T[None, None, :, :, None]
    return u * mixed.reshape(b, t, D_B)


def encoder_layer(x, ffn1_norm, ffn1_w_in, ffn1_w_out, mix_norm, w_in_mix, q_norm, k_norm,
                  attn_rpb, gate_norm, w_spatial, b_spatial, out_norm_a, out_norm_b, w_out_mix,
                  ffn2_norm, ffn2_w_in, ffn2_w_out, final_norm):
    b, t, _ = x.shape
    x = x + 0.5 * swiglu(rmsnorm(x, ffn1_norm), ffn1_w_in, ffn1_w_out)
    h = rmsnorm(x, mix_norm)
    z = h @ w_in_mix
    q, k, v, u, g = jnp.split(z, [D_A, 2 * D_A, 3 * D_A, 3 * D_A + D_B], axis=-1)
    q = rmsnorm(q.reshape(b, t, H_A, DH_A), q_norm)
    k = rmsnorm(k.reshape(b, t, H_A, DH_A), k_norm)
    v = v.reshape(b, t, H_A, DH_A)
    a = neighbourhood_attention(q, k, v, attn_rpb)
    sg = spatial_gating(jax.nn.gelu(u, approximate=False), jax.nn.gelu(g, approximate=False),
                        gate_norm, w_spatial, b_spatial)
    o = jnp.concatenate([rmsnorm(a, out_norm_a), rmsnorm(sg, out_norm_b)], axis=-1) @ w_out_mix
    x = x + o
    x = x + 0.5 * swiglu(rmsnorm(x, ffn2_norm), ffn2_w_in, ffn2_w_out)
    return rmsnorm(x, final_norm)


def setup_inputs(seed: int = 0) -> dict:
    key = jax.random.key(seed)
    ks = jax.random.split(key, 22)
    f32 = jnp.float32

    def nrm(k, shape, scale):
        return jax.random.normal(k, shape, f32) * scale

    def gain(k, shape):
        return 1.0 + 0.05 * jax.random.normal(k, shape, f32)

    return {
        "x_prompt": nrm(ks[0], (BATCH, SEQ, D_MODEL), 1.0),
        "x_sample": nrm(ks[1], (DEC_BATCH, DEC_SEQ, D_MODEL), 1.0),
        "ffn1_norm": gain(ks[2], (DEPTH, D_MODEL)),
        "ffn1_w_in": nrm(ks[3], (DEPTH, D_MODEL, 2 * D_FF), D_MODEL ** -0.5),
        "ffn1_w_out": nrm(ks[4], (DEPTH, D_FF, D_MODEL), D_FF ** -0.5),
        "mix_norm": gain(ks[5], (DEPTH, D_MODEL)),
        "w_in_mix": nrm(ks[6], (DEPTH, D_MODEL, D_IN_MIX), D_MODEL ** -0.5),
        "q_norm": gain(ks[7], (DEPTH, DH_A)),
        "k_norm": gain(ks[8], (DEPTH, DH_A)),
        "attn_rpb": nrm(ks[9], (DEPTH, H_A, 2 * NA_KH - 1, 2 * NA_KW - 1), 0.1),
        "gate_norm": gain(ks[10], (DEPTH, D_B)),
        "w_spatial": nrm(ks[11], (DEPTH, H_B, CHUNK, CHUNK), CHUNK ** -0.5),
        "b_spatial": gain(ks[12], (DEPTH, H_B, CHUNK)),
        "out_norm_a": gain(ks[13], (DEPTH, D_A)),
        "out_norm_b": gain(ks[14], (DEPTH, D_B)),
        "w_out_mix": nrm(ks[15], (DEPTH, D_MIX, D_MODEL), D_MIX ** -0.5),
        "ffn2_norm": gain(ks[16], (DEPTH, D_MODEL)),
        "ffn2_w_in": nrm(ks[17], (DEPTH, D_MODEL, 2 * D_FF), D_MODEL ** -0.5),
        "ffn2_w_out": nrm(ks[18], (DEPTH, D_FF, D_MODEL), D_FF ** -0.5),
        "final_norm": gain(ks[19], (DEPTH, D_MODEL)),
    }


def reference(x_prompt, x_sample, ffn1_norm, ffn1_w_in, ffn1_w_out, mix_norm, w_in_mix, q_norm,
              k_norm, attn_rpb, gate_norm, w_spatial, b_spatial, out_norm_a, out_norm_b, w_out_mix,
              ffn2_norm, ffn2_w_in, ffn2_w_out, final_norm):
    def run(x):
        for l in range(DEPTH):
            x = encoder_layer(x, ffn1_norm[l], ffn1_w_in[l], ffn1_w_out[l], mix_norm[l], w_in_mix[l],
                              q_norm[l], k_norm[l], attn_rpb[l], gate_norm[l], w_spatial[l],
                              b_spatial[l], out_norm_a[l], out_norm_b[l], w_out_mix[l],
                              ffn2_norm[l], ffn2_w_in[l], ffn2_w_out[l], final_norm[l])
        return x

    y_prompt = run(x_prompt)
    y_sample = run(x_sample)
    return (y_prompt, y_sample)
```

```python
import numpy as np
import concourse.bass as bass
import concourse.mybir as mybir
from concourse.bass_utils import run_bass_kernel_spmd

F32 = mybir.dt.float32
BF16 = mybir.dt.bfloat16
AF = mybir.ActivationFunctionType
ALU = mybir.AluOpType

D = 2048
C = 16
DFF = 5632
HC = 44
T = 512
NEG = -30000.0
EPS = 1e-6
NCORES = 8
NT_FULL = 12
STRICT_SAME_ENGINE = True
MASK_ENG = "dve"


class Tok:
    __slots__ = ("kind", "sem", "val", "key")

    def __init__(self, kind, sem, val, key):
        self.kind, self.sem, self.val, self.key = kind, sem, val, key


class Res:
    __slots__ = ("w", "r", "name")

    def __init__(self, name=""):
        self.w = None
        self.r = {}
        self.name = name


class Sched:
    NDMA = 24
    SEM_MAX = 4000

    def __init__(self, nc):
        self.nc = nc
        self.eng = {"pe": nc.tensor, "act": nc.scalar, "dve": nc.vector, "pool": nc.gpsimd, "sp": nc.sync}
        self.sem = {}
        self.cnt = {}
        self.semgen = {}
        self.nsem = 0
        for e in self.eng:
            self.semgen[e] = 0
            self._new_sem(e)
        self.waited = {}
        self.dma_sems = [nc.alloc_semaphore(f"dq{i}") for i in range(self.NDMA)]
        self.dma_cnt = [0] * self.NDMA
        self.dma_rr = 0
        self.dma_rr_q = {}
        self.last = {}
        self.n_inst = 0
        self.n_wait = 0

    def _new_sem(self, e):
        self.semgen[e] += 1
        self.sem[e] = self.nc.alloc_semaphore(f"s_{e}_{self.semgen[e]}")
        self.cnt[e] = 0

    def _wait(self, e, tok):
        if tok is None:
            return
        k = (e, tok.key)
        if self.waited.get(k, 0) >= tok.val:
            return
        self.eng[e].wait_ge(tok.sem, tok.val)
        self.waited[k] = tok.val
        self.n_wait += 1

    def _deps(self, e, reads, writes):
        for r in reads:
            if r.w is not None:
                self._dep1(e, r.w)
        for w in writes:
            if w.w is not None:
                self._dep1(e, w.w)
            for t in w.r.values():
                self._dep1(e, t)

    def _dep1(self, e, t):
        if t.kind == e:
            if e == "pe" or not STRICT_SAME_ENGINE:
                return
        self._wait(e, t)

    def op(self, e, fn, reads=(), writes=()):
        self._deps(e, reads, writes)
        inst = fn(self.eng[e])
        if self.cnt[e] >= self.SEM_MAX:
            self._new_sem(e)
        self.cnt[e] += 1
        inst.then_inc(self.sem[e], 1)
        tok = Tok(e, self.sem[e], self.cnt[e], (e, self.semgen[e]))
        for r in reads:
            r.r[e] = tok
        for w in writes:
            w.w = tok
            w.r = {}
        self.last[e] = tok
        self.n_inst += 1
        return tok

    def dma(self, q, out, in_, reads=(), writes=()):
        self._deps(q, reads, writes)
        lo, hi = (0, 8) if q == "sp" else (8, self.NDMA)
        i = self.dma_rr_q.get(q, lo)
        self.dma_rr_q[q] = lo + (i + 1 - lo) % (hi - lo)
        key = ("dma", i)
        if self.dma_cnt[i] > 0:
            self._wait(q, Tok("dma", self.dma_sems[i], self.dma_cnt[i] * 16, key))
        inst = self.eng[q].dma_start(out=out, in_=in_)
        self.dma_cnt[i] += 1
        inst.then_inc(self.dma_sems[i], 16)
        tok = Tok("dma", self.dma_sems[i], self.dma_cnt[i] * 16, key)
        for r in reads:
            r.r[key] = tok
        for w in writes:
            w.w = tok
            w.r = {}
        self.n_inst += 1
        return tok

    def all_tokens(self):
        toks = list(self.last.values())
        for i in range(self.NDMA):
            if self.dma_cnt[i] > 0:
                toks.append(Tok("dma", self.dma_sems[i], self.dma_cnt[i] * 16, ("dma", i)))
        return toks

    def barrier(self, engines=None):
        toks = self.all_tokens()
        for e in engines or list(self.eng):
            for t in toks:
                if t.kind == e:
                    continue
                self._wait(e, t)

    def finish(self):
        toks = self.all_tokens()
        for t in toks:
            if t.kind != "sp":
                self._wait("sp", t)


def build_program(NT):
    nc = bass.Bass("TRN2", target_bir_lowering=False)
    S = Sched(nc)
    NTA = NT + 1
    NTOK_EXT = NT * T + 512

    def dram_in(name, shape, dt=F32):
        return nc.dram_tensor(name, list(shape), dt, kind="ExternalInput").ap()

    def dram_scr(name, shape, dt):
        return nc.dram_tensor(name, list(shape), dt, kind="Internal").ap()

    xin = dram_in("xin", [NTA * T, D])
    w1i = dram_in("w1i", [D, 2 * DFF])
    w1o = dram_in("w1o", [DFF, D])
    wmi = dram_in("wmi", [D, 5120])
    wmo = dram_in("wmo", [D, D])
    w2i = dram_in("w2i", [D, 2 * DFF])
    w2o = dram_in("w2o", [DFF, D])
    gv_d = dram_in("gv", [128, 82])
    ggb_d = dram_in("ggb", [128, 1024])
    hmask_d = dram_in("hmask", [128, 2048])
    wsT_d = dram_in("wsT", [128, 16 * 128])
    bsT_d = dram_in("bsT", [128, 8 * 128])
    rpbT_d = dram_in("rpbT", [128, 240])
    bandc_d = dram_in("bandc", [128, 64 * 128])
    colmask_d = dram_in("colmask", [128, 64])
    wmask_d = dram_in("wmask", [128, NT * 64])
    ident_d = dram_in("ident", [128, 128])
    y = nc.dram_tensor("y", [NT * T, D], F32, kind="ExternalOutput").ap()

    win_s = [dram_scr(f"win_s{f}", [22, 128, 8192], BF16) for f in range(2)]
    wout_s = [dram_scr(f"wout_s{f}", [16, 128, 5632], BF16) for f in range(2)]
    wfm_s = dram_scr("wfm_s", [12, 128, 4096], BF16)
    wtm_s = dram_scr("wtm_s", [4, 128, 8192], BF16)
    wmo_s = dram_scr("wmo_s", [4, 128, 8192], BF16)
    tbl_s = dram_scr("tbl_s", [16, 128, 1408], BF16)
    x1_s = dram_scr("x1_s", [NT, 128, C * T], F32)
    q_s = dram_scr("q_s", [NT, 128, 8 * T], BF16)
    sg_s = dram_scr("sg_s", [NT, 128, 8 * T], BF16)
    k_s = dram_scr("k_s", [128, 8, NTOK_EXT], BF16)
    v_s = dram_scr("v_s", [NTOK_EXT, 1024], BF16)
    x1_res = [Res() for _ in range(NT)]
    q_res = [Res() for _ in range(NT)]
    sg_res = [Res() for _ in range(NT)]
    kv_res = [Res() for _ in range(2 * NT + 2)]

    arena_cms = {}

    def arena(name, nbytes):
        cm = nc.sbuf_tensor(name, [128, nbytes // 4], F32)
        arena_cms[name] = cm
        return cm.__enter__()

    def view(ar, off, dt, shape):
        n = 1
        for s_ in shape:
            n *= s_
        if dt == F32:
            ap = ar[:, off // 4: off // 4 + n]
        else:
            ap = ar[:, off // 4: off // 4 + n // 2].bitcast(BF16)
        if len(shape) == 2:
            ap = ap.rearrange("p (a b) -> p a b", a=shape[0], b=shape[1])
        elif len(shape) == 3:
            ap = ap.rearrange("p (a b c) -> p a b c", a=shape[0], b=shape[1], c=shape[2])
        return ap

    psum = [nc.psum_tensor(f"ps{i}", [128, 512], F32).__enter__()[:, :] for i in range(8)]
    ps_res = [Res(f"ps{i}") for i in range(8)]
    ps_rr = {"all": 0, "s": 0, "o": 0}
    PS_POOLS = {"all": list(range(8)), "s": [0, 1, 2, 3, 4], "o": [5, 6, 7]}

    def ps_get(pool="all"):
        lst = PS_POOLS[pool]
        b = lst[ps_rr[pool] % len(lst)]
        ps_rr[pool] += 1
        return psum[b], ps_res[b]

    CONST = arena("const", 19 * 1024)
    co = [0]

    def calloc(dt, shape):
        n = 1
        for s_ in shape:
            n *= s_
        nb = n * (4 if dt == F32 else 2)
        nb = (nb + 31) // 32 * 32
        ap = view(CONST, co[0], dt, shape)
        co[0] += nb
        assert co[0] <= 19 * 1024, co[0]
        return ap

    ident = calloc(F32, [128])
    onesf = calloc(F32, [128])
    blockones = calloc(F32, [128])
    onesb = calloc(BF16, [128])
    gv = calloc(F32, [82])
    ggbE = calloc(F32, [1024])
    ggbO = calloc(F32, [1024])
    wsT = calloc(BF16, [16, 128])
    bsT = calloc(F32, [8, 128])
    epsb = calloc(F32, [1])
    identb = calloc(BF16, [128])
    const_res = Res("const")

    PRE = arena("pre", 160 * 1024)
    NSTG = 3
    stg_f = [view(PRE, i * 16384, F32, [4096]) for i in range(NSTG)]
    stg_b = [view(PRE, 49152 + i * 8192, BF16, [4096]) for i in range(NSTG)]
    stg_f_res = [Res() for _ in range(NSTG)]
    stg_b_res = [Res() for _ in range(NSTG)]
    poff = 49152 + NSTG * 8192
    tmpf = view(PRE, poff, F32, [2048])
    tmpf_res = Res()
    poff += 8192
    TB = view(PRE, poff, BF16, [16 * 22 * 64])
    TB_res = Res()
    poff += 45056
    rpbT = view(PRE, poff, F32, [240])
    poff += 960
    colmask = view(PRE, poff, F32, [64])
    poff += 256
    bandc = view(PRE, poff, F32, [64 * 128])
    poff += 32768
    assert poff <= 160 * 1024, poff
    pre_res = Res()

    S.dma("sp", ident, ident_d, writes=[const_res])
    S.dma("sp", gv, gv_d, writes=[const_res])
    S.dma("sp", bsT.rearrange("p a b -> p (a b)"), bsT_d, writes=[const_res])
    S.dma("sp", rpbT, rpbT_d, writes=[pre_res])
    S.dma("sp", colmask, colmask_d, writes=[pre_res])
    S.dma("sp", bandc, bandc_d, writes=[pre_res])
    S.op("dve", lambda e: e.tensor_copy(out=identb, in_=ident), reads=[const_res], writes=[const_res])
    S.op("dve", lambda e: e.memset(onesf, 1.0), writes=[const_res])
    S.op("dve", lambda e: e.memset(onesb, 1.0), writes=[const_res])
    S.op("dve", lambda e: e.memset(epsb, EPS), writes=[const_res])
    S.op("dve", lambda e: e.memset(blockones, 0.0), writes=[const_res])
    S.op("dve", lambda e: e.memset(blockones[0:64, 0:64], 1.0), writes=[const_res])
    S.op("dve", lambda e: e.memset(blockones[64:128, 64:128], 1.0), writes=[const_res])
    S.op("dve", lambda e: e.tensor_scalar(out=gv[:, 80:81], in0=gv[:, 80:81], scalar1=0.125, scalar2=None,
                                          op0=ALU.mult), reads=[const_res], writes=[const_res])
    S.dma("sp", tmpf[:, 0:1024], ggb_d, writes=[tmpf_res])
    S.dma("sp", stg_f[0][:, 0:2048], hmask_d, writes=[stg_f_res[0]])
    S.op("dve", lambda e: e.tensor_tensor(out=ggbE, in0=tmpf[:, 0:1024], in1=stg_f[0][:, 0:1024], op=ALU.mult),
         reads=[tmpf_res, stg_f_res[0]], writes=[const_res])
    S.op("dve", lambda e: e.tensor_tensor(out=ggbO, in0=tmpf[:, 0:1024], in1=stg_f[0][:, 1024:2048], op=ALU.mult),
         reads=[tmpf_res, stg_f_res[0]], writes=[const_res])
    S.dma("sp", tmpf[:, 0:2048], wsT_d, reads=[], writes=[tmpf_res])
    S.op("dve", lambda e: e.tensor_copy(out=wsT.rearrange("p a b -> p (a b)"), in_=tmpf[:, 0:2048]),
         reads=[tmpf_res], writes=[const_res])

    S.op("dve", lambda e: e.memset(TB, NEG), writes=[TB_res])
    TB4 = TB.rearrange("p (h e c) -> p h e c", h=16, e=22, c=64)
    for c in range(64):
        ps, pr = ps_get()
        S.op("pe", lambda e, ps=ps, c=c: e.matmul(ps[:, 0:240], bandc[:, c * 128:(c + 1) * 128], rpbT[:, 0:240],
                                                   start=True, stop=True),
             reads=[pre_res], writes=[pr])
        src = ps[:, 0:240].rearrange("p (h e) -> p h e", h=16, e=15)
        eng = "act" if c % 2 == 0 else "dve"
        if eng == "act":
            S.op("act", lambda e, src=src, c=c: e.activation(out=TB4[0:64, :, 3:18, c], in_=src[0:64], func=AF.Copy),
                 reads=[pr], writes=[TB_res])
            S.op("act", lambda e, src=src, c=c: e.activation(out=TB4[64:128, :, 4:19, c], in_=src[64:128], func=AF.Copy),
                 reads=[pr], writes=[TB_res])
        else:
            S.op("dve", lambda e, src=src, c=c: e.tensor_copy(out=TB4[0:64, :, 3:18, c], in_=src[0:64]),
                 reads=[pr], writes=[TB_res])
            S.op("dve", lambda e, src=src, c=c: e.tensor_copy(out=TB4[64:128, :, 4:19, c], in_=src[64:128]),
                 reads=[pr], writes=[TB_res])
    TB3 = TB.rearrange("p (m c) -> p m c", m=352, c=64)
    S.op("dve", lambda e: e.tensor_tensor(out=TB3, in0=TB3, in1=colmask.unsqueeze(1).broadcast_to([128, 352, 64]),
                                          op=ALU.add), reads=[pre_res, TB_res], writes=[TB_res])
    S.dma("pool", tbl_s.rearrange("h p f -> p h f"), TB.rearrange("p (h f) -> p h f", h=16), reads=[TB_res])

    cv = [0]

    def convert(src, dst):
        i = cv[0] % NSTG
        eng = "act" if cv[0] % 2 == 0 else "dve"
        cv[0] += 1
        a, b = src.shape[1], src.shape[2]
        sf = stg_f[i].rearrange("p (a b) -> p a b", a=a, b=b)
        S.dma("sp", sf, src, writes=[stg_f_res[i]])
        if eng == "act":
            S.op("act", lambda e: e.activation(out=stg_b[i], in_=stg_f[i], func=AF.Copy),
                 reads=[stg_f_res[i]], writes=[stg_b_res[i]])
        else:
            S.op("dve", lambda e: e.tensor_copy(out=stg_b[i], in_=stg_f[i]),
                 reads=[stg_f_res[i]], writes=[stg_b_res[i]])
        S.dma("pool", dst, stg_b[i], reads=[stg_b_res[i]])

    def conv_ffn_in(w, ws):
        r = w.rearrange("(kc p) (gu g col) -> g gu p kc col", p=128, gu=2, col=256)
        for g in range(22):
            for gu in range(2):
                convert(r[g, gu], ws[g, :, gu * 4096:(gu + 1) * 4096])

    def conv_ffn_out(w, ws):
        r = w.rearrange("(j p) (m col) -> m p j col", p=128, col=128)
        for m in range(16):
            for hf in range(2):
                src = r[m][:, hf * 22:(hf + 1) * 22, :]
                i = cv[0] % NSTG
                eng = "act" if cv[0] % 2 == 0 else "dve"
                cv[0] += 1
                sf = stg_f[i][:, 0:2816].rearrange("p (a b) -> p a b", a=22, b=128)
                S.dma("sp", sf, src, writes=[stg_f_res[i]])
                if eng == "act":
                    S.op("act", lambda e, i=i: e.activation(out=stg_b[i][:, 0:2816], in_=stg_f[i][:, 0:2816], func=AF.Copy),
                         reads=[stg_f_res[i]], writes=[stg_b_res[i]])
                else:
                    S.op("dve", lambda e, i=i: e.tensor_copy(out=stg_b[i][:, 0:2816], in_=stg_f[i][:, 0:2816]),
                         reads=[stg_f_res[i]], writes=[stg_b_res[i]])
                S.dma("pool", ws[m, :, hf * 2816:(hf + 1) * 2816], stg_b[i][:, 0:2816], reads=[stg_b_res[i]])

    conv_ffn_in(w1i, win_s[0])
    conv_ffn_out(w1o, wout_s[0])
    FM_COL0 = {"q": 0, "k": 1024, "u": 3072}
    fm_groups = [(kd, g) for kd in ("q", "k", "u") for g in range(4)]
    for idx, (kd, g) in enumerate(fm_groups):
        c0 = FM_COL0[kd] + g * 256
        convert(wmi[:, c0:c0 + 256].rearrange("(kc p) col -> p kc col", p=128), wfm_s[idx])
    for idx in range(4):
        c0 = (2048 if idx < 2 else 4096) + (idx % 2) * 512
        r = wmi[:, c0:c0 + 512].rearrange("(kc p) col -> p kc col", p=128)
        for hf in range(2):
            convert(r[:, hf * 8:(hf + 1) * 8, :], wtm_s[idx][:, hf * 4096:(hf + 1) * 4096])
    deferred = []
    r_ = w2i.rearrange("(kc p) (gu g col) -> g gu p kc col", p=128, gu=2, col=256)
    for g in range(22):
        for gu in range(2):
            for hf in range(2):
                deferred.append((r_[g, gu][:, hf * 8:(hf + 1) * 8, :],
                                 win_s[1][g, :, gu * 4096 + hf * 2048: gu * 4096 + (hf + 1) * 2048]))
    for g in range(4):
        r = wmo[:, g * 512:(g + 1) * 512].rearrange("(kc p) col -> p kc col", p=128)
        for q4 in range(4):
            deferred.append((r[:, q4 * 4:(q4 + 1) * 4, :], wmo_s[g][:, q4 * 2048:(q4 + 1) * 2048]))
    r_ = w2o.rearrange("(j p) (m col) -> m p j col", p=128, col=128)
    for m in range(16):
        for (j0, j1) in ((0, 16), (16, 32), (32, 44)):
            deferred.append((r_[m][:, j0:j1, :], wout_s[1][m, :, j0 * 128:j1 * 128]))

    S.barrier()
    arena_cms["pre"].__exit__(None, None, None)

    XA = arena("xa", 32768)
    HA = arena("ha", 16384)
    HIDA = arena("hida", 45056)
    WRA = arena("wra", 3 * 16384)
    MISC = arena("misc", 46 * 1024)

    X = view(XA, 0, F32, [C, T])
    X_res = [Res(f"x{c}") for c in range(C)]
    H = view(HA, 0, BF16, [C, T])
    AT = view(HA, 0, F32, [8, T])
    H_res = [Res(f"h{c}") for c in range(C)]
    HR = [[r] for r in H_res]
    XR = [[r] for r in X_res]
    HID = view(HIDA, 0, BF16, [HC, T])
    seg = [Res(f"seg{i}") for i in range(6)]
    SEG_ALL = seg
    XTOK = [view(HA, 0, F32, [2048]), view(HA, 8192, F32, [2048])]
    XTOK_res = [H_res[0:8], H_res[8:16]]
    GU = view(HIDA, 0, BF16, [8, T])
    QN = view(HIDA, 8192, BF16, [8, T])
    KN = view(HIDA, 16384, BF16, [8, T])
    SG = view(HIDA, 24576, F32, [8, T])
    Q = view(HIDA, 0, BF16, [8, T])
    K = view(HIDA, 8192, BF16, [8, 2 * T])
    V = view(HIDA, 24576, BF16, [8, 1024])
    AN = view(HIDA, 0, BF16, [8, T])
    QH = [view(HIDA, 40960 + k * 1024, BF16, [T]) for k in range(4)]
    YTOK = [view(HA, 0, F32, [2048]), view(HA, 8192, F32, [2048])]
    YTOK_res = [H_res[0:8], H_res[8:16]]

    ring = [WRA[:, i * 4096:(i + 1) * 4096].bitcast(BF16) for i in range(3)]
    ring_res = [Res(f"ring{i}") for i in range(3)]
    rr = [0]

    def ring_load(src2d, n):
        i = rr[0] % 3
        rr[0] += 1
        S.dma("sp", ring[i][:, 0:n], src2d, writes=[ring_res[i]])
        return ring[i], ring_res[i]

    mo = [0]

    def malloc(dt, shape):
        n = 1
        for s_ in shape:
            n *= s_
        nb = n * (4 if dt == F32 else 2)
        nb = (nb + 31) // 32 * 32
        ap = view(MISC, mo[0], dt, shape)
        mo[0] += nb
        assert mo[0] <= 46 * 1024, mo[0]
        return ap

    SQ = [malloc(F32, [T]) for _ in range(2)]
    SQ_res = [Res() for _ in range(2)]
    ACC = malloc(F32, [T])
    ACC_res = Res()
    RSTD = malloc(F32, [T])
    RSTD_res = Res()
    STMP = [malloc(F32, [T]) for _ in range(2)]
    STMP_res = [Res() for _ in range(2)]
    PT = [malloc(BF16, [T]) for _ in range(3)]
    PT_res = [Res() for _ in range(3)]
    PT += [STMP[0][:, 0:256].bitcast(BF16), STMP[0][:, 256:512].bitcast(BF16)]
    PT_res += [STMP_res[0], STMP_res[0]]
    RSC = STMP[1]
    RSC_res = STMP_res[1]
    P2 = PT
    P2_res = PT_res
    vh_off = mo[0]
    GG = malloc(F32, [1024])
    GG_res = Res()
    GNE = malloc(BF16, [1024])
    GNO = malloc(BF16, [1024])
    GN_res = Res()
    JUNK = GNE
    JUNK_res = GN_res
    VH = [view(MISC, vh_off + k * 2048, BF16, [8, 128]) for k in range(4)]
    VH_res = [Res() for _ in range(4)]
    VTOK = [malloc(BF16, [1024]) for _ in range(2)]
    VTOK_res = [Res() for _ in range(2)]
    csf_off = mo[0]
    SGN = malloc(BF16, [8, T])
    SGN_res = Res()
    BT = [malloc(BF16, [1408]) for _ in range(2)]
    BT_res = [Res() for _ in range(2)]
    RDEN = malloc(F32, [T])
    RDEN_res = Res()
    SSQ = malloc(F32, [8])
    SSQ_res = Res()
    WM = [malloc(F32, [64]) for _ in range(2)]
    WM_res = [Res() for _ in range(2)]
    GTMP = malloc(F32, [4, 128])
    GTMP_res = Res()

    CSF = view(MISC, csf_off, F32, [2048])
    CSB = view(MISC, csf_off + 8192, BF16, [2048])
    CSB_res = [BT_res[0], BT_res[1]]
    conv = {"k": 0, "step": 0, "quota": 0}

    def conv_step():
        if conv["k"] >= len(deferred) or conv["quota"] <= 0:
            return
        src, dst = deferred[conv["k"]]
        a, b = src.shape[1], src.shape[2]
        n = a * b
        if conv["step"] == 0:
            S.dma("pool", CSF[:, 0:n].rearrange("p (a b) -> p a b", a=a, b=b), src, writes=[SGN_res])
            conv["step"] = 1
        elif conv["step"] == 1:
            if conv["k"] % 2 == 0:
                S.op("act", lambda e: e.activation(out=CSB[:, 0:n], in_=CSF[:, 0:n], func=AF.Copy),
                     reads=[SGN_res], writes=CSB_res)
            else:
                S.op("dve", lambda e: e.tensor_copy(out=CSB[:, 0:n], in_=CSF[:, 0:n]),
                     reads=[SGN_res], writes=CSB_res)
            conv["step"] = 2
        else:
            S.dma("pool", dst, CSB[:, 0:n], reads=CSB_res)
            conv["step"] = 0
            conv["k"] += 1
            conv["quota"] -= 1

    def norm_stat(c, x_ap, xr):
        if c == 0:
            S.op("act", lambda e: e.activation(out=ACC, in_=x_ap, func=AF.Square), reads=xr, writes=[ACC_res])
        else:
            sq, sr = SQ[c % 2], SQ_res[c % 2]
            S.op("act", lambda e: e.activation(out=sq, in_=x_ap, func=AF.Square), reads=xr, writes=[sr])
            S.op("dve", lambda e: e.tensor_tensor(out=ACC, in0=ACC, in1=sq, op=ALU.add),
                 reads=[sr, ACC_res], writes=[ACC_res])

    def norm_finish(xs, xres, gcol0, outs, ores, Dn):
        n = len(xs)
        ps, pr = ps_get()
        S.op("pe", lambda e: e.matmul(ps, onesf, ACC, start=True, stop=True), reads=[ACC_res, const_res], writes=[pr])
        S.op("act", lambda e: e.activation(out=RSTD, in_=ps, func=AF.Ln, bias=epsb[:, 0:1], scale=1.0 / Dn),
             reads=[pr, const_res], writes=[RSTD_res])
        S.op("act", lambda e: e.activation(out=RSTD, in_=RSTD, func=AF.Exp, scale=-0.5), reads=[RSTD_res], writes=[RSTD_res])
        for c in range(n):
            S.op("dve", lambda e, c=c: e.scalar_tensor_tensor(out=outs[c], in0=xs[c], scalar=gv[:, gcol0 + c:gcol0 + c + 1],
                                                            in1=RSTD, op0=ALU.mult, op1=ALU.mult),
                 reads=xres[c] + [RSTD_res, const_res], writes=ores[c])

    def fm_norm(xs, xres, gcol0, outs, ores, Dn, stats_done=False):
        if not stats_done:
            for c in range(len(xs)):
                norm_stat(c, xs[c], xres[c])
        norm_finish(xs, xres, gcol0, outs, ores, Dn)

    Xc = [X[:, c, :] for c in range(C)]
    Hc = [H[:, c, :] for c in range(C)]

    def ffn(f):
        for g in range(22):
            if f == 0:
                conv_step()
            slot, sres = ring_load(win_s[f][g], 8192)
            sv = slot.rearrange("p (gu kc col) -> p gu kc col", gu=2, kc=16, col=256)
            for s_ in range(2):
                j = 2 * g + s_
                psg, prg = ps_get()
                psu, pru = ps_get()
                for kc in range(16):
                    S.op("pe", lambda e, kc=kc: e.matmul(psg, sv[:, 0, kc, s_ * 128:(s_ + 1) * 128], H[:, kc, :],
                                                         start=(kc == 0), stop=(kc == 15)),
                         reads=[sres, H_res[kc]], writes=[prg])
                for kc in range(16):
                    S.op("pe", lambda e, kc=kc: e.matmul(psu, sv[:, 1, kc, s_ * 128:(s_ + 1) * 128], H[:, kc, :],
                                                         start=(kc == 0), stop=(kc == 15)),
                         reads=[sres, H_res[kc]], writes=[pru])
                st, sr = STMP[j % 2], STMP_res[j % 2]
                S.op("act", lambda e: e.activation(out=st, in_=psg, func=AF.Silu), reads=[prg], writes=[sr])
                S.op("dve", lambda e: e.tensor_tensor(out=HID[:, j, :], in0=st, in1=psu, op=ALU.mult),
                     reads=[sr, pru], writes=[seg[j // 8]])
        for m in range(16):
            if f == 0:
                conv_step()
            slot, sres = ring_load(wout_s[f][m], 5632)
            sv = slot[:, 0:5632].rearrange("p (j col) -> p j col", j=44, col=128)
            ps, pr = ps_get()
            for j in range(HC):
                S.op("pe", lambda e, j=j: e.matmul(ps, sv[:, j, :], HID[:, j, :], start=(j == 0), stop=(j == HC - 1)),
                     reads=[sres, seg[j // 8]], writes=[pr])
            S.op("dve", lambda e: e.scalar_tensor_tensor(out=Xc[m], in0=ps, scalar=0.5, in1=Xc[m],
                                                         op0=ALU.mult, op1=ALU.add),
                 reads=[pr, X_res[m]], writes=[X_res[m]])
            norm_stat(m, Xc[m], [X_res[m]])

    def prefetch_x0(i):
        S.dma("pool", CSF, xin[i * T: i * T + 128, :], writes=[SGN_res])

    def load_x(i, blk0_in_csf=False):
        for blk in range(4):
            if blk == 0 and blk0_in_csf:
                xt, xr = CSF, [SGN_res]
            else:
                xt, xr = XTOK[blk % 2], XTOK_res[blk % 2]
                S.dma("pool", xt, xin[i * T + blk * 128: i * T + (blk + 1) * 128, :], writes=xr)
            for cg in range(4):
                ps, pr = ps_get()
                for cc in range(4):
                    c = cg * 4 + cc
                    S.op("pe", lambda e, c=c, cc=cc: e.transpose(ps[:, cc * 128:(cc + 1) * 128],
                                                                 xt[:, c * 128:(c + 1) * 128], ident),
                         reads=xr + [const_res], writes=[pr])
                dst = X[:, cg * 4:(cg + 1) * 4, blk * 128:(blk + 1) * 128]
                src = ps.rearrange("p (a b) -> p a b", a=4, b=128)
                if (blk * 4 + cg) % 2 == 0:
                    S.op("act", lambda e: e.activation(out=dst, in_=src, func=AF.Copy),
                         reads=[pr], writes=X_res[cg * 4:(cg + 1) * 4])
                else:
                    S.op("dve", lambda e: e.tensor_copy(out=dst, in_=src),
                         reads=[pr], writes=X_res[cg * 4:(cg + 1) * 4])

    def stage_a(i, halo, prefetch_next):
        conv["quota"] = 12
        ffn(0)
        while halo and conv["k"] < len(deferred):
            conv["quota"] = 1
            conv_step()
        if not halo:
            S.dma("pool", x1_s[i], X.rearrange("p a b -> p (a b)"), reads=X_res, writes=[x1_res[i]])
        if prefetch_next is not None:
            prefetch_x0(prefetch_next)
        fm_norm(Xc, XR, 16, Hc, HR, D, stats_done=True)
        chain = []

        def qk_chain(ps, pr, sq, sr, kd, c):
            ps2, pr2 = ps_get()
            S.op("pe", lambda e: e.matmul(ps2, blockones, sq, start=True, stop=True),
                 reads=[sr, const_res], writes=[pr2])
            S.op("act", lambda e: e.activation(out=RSTD, in_=ps2, func=AF.Ln, bias=epsb[:, 0:1], scale=1.0 / 64),
                 reads=[pr2, const_res], writes=[RSTD_res])
            S.op("act", lambda e: e.activation(out=RSTD, in_=RSTD, func=AF.Exp, scale=-0.5), reads=[RSTD_res], writes=[RSTD_res])
            dstb = QN if kd == "q" else KN
            dres = seg[1] if kd == "q" else seg[2]
            gcol = 80 if kd == "q" else 81
            S.op("dve", lambda e: e.scalar_tensor_tensor(out=dstb[:, c, :], in0=ps, scalar=gv[:, gcol:gcol + 1],
                                                         in1=RSTD, op0=ALU.mult, op1=ALU.mult),
                 reads=[pr, RSTD_res, const_res], writes=[dres])

        for idx, (kd, g) in enumerate(fm_groups):
            if halo and kd != "k":
                continue
            slot, sres = ring_load(wfm_s[idx], 4096)
            sv = slot[:, 0:4096].rearrange("p (kc col) -> p kc col", kc=16, col=256)
            for s_ in range(2):
                c = 2 * g + s_
                ps, pr = ps_get()
                for kc in range(16):
                    S.op("pe", lambda e, kc=kc: e.matmul(ps, sv[:, kc, s_ * 128:(s_ + 1) * 128], H[:, kc, :],
                                                         start=(kc == 0), stop=(kc == 15)),
                         reads=[sres, H_res[kc]], writes=[pr])
                if kd == "u":
                    while chain:
                        qk_chain(*chain.pop(0))
                    S.op("act", lambda e: e.activation(out=GU[:, c, :], in_=ps, func=AF.Gelu), reads=[pr], writes=[seg[0]])
                else:
                    sq, sr = SQ[c % 2], SQ_res[c % 2]
                    S.op("act", lambda e: e.activation(out=sq, in_=ps, func=AF.Square), reads=[pr], writes=[sr])
                    if chain:
                        qk_chain(*chain.pop(0))
                    chain.append((ps, pr, sq, sr, kd, c))
        while chain:
            qk_chain(*chain.pop(0))
        if not halo:
            S.dma("pool", q_s[i], QN.rearrange("p a b -> p (a b)"), reads=[seg[1]], writes=[q_res[i]])
            t0 = 256 + i * T
            S.dma("pool", k_s[:, :, t0:t0 + T], KN, reads=[seg[2]], writes=[kv_res[2 * i + 1], kv_res[2 * i + 2]])
        else:
            S.dma("pool", k_s[:, :, 0:256], KN[:, :, 0:256], reads=[seg[2]], writes=[kv_res[0]])
            S.dma("pool", k_s[:, :, NTOK_EXT - 256:NTOK_EXT], KN[:, :, 256:512], reads=[seg[2]],
                  writes=[kv_res[2 * NT + 1]])
        s0, r0 = ring_load(wtm_s[0], 8192)
        s1, r1 = ring_load(wtm_s[1], 8192)
        svs = [s0.rearrange("p (kc col) -> p kc col", kc=16, col=512), s1.rearrange("p (kc col) -> p kc col", kc=16, col=512)]
        srs = [r0, r1]
        for blk in range(4):
            vt, vr = VTOK[blk % 2], VTOK_res[blk % 2]
            for t in range(2):
                ps, pr = ps_get()
                for kc in range(16):
                    S.op("pe", lambda e, kc=kc: e.matmul(ps, H[:, kc, blk * 128:(blk + 1) * 128], svs[t][:, kc, :],
                                                         start=(kc == 0), stop=(kc == 15)),
                         reads=[srs[t], H_res[kc]], writes=[pr])
                if t == 0:
                    S.op("act", lambda e: e.activation(out=vt[:, 0:512], in_=ps, func=AF.Copy), reads=[pr], writes=[vr])
                else:
                    S.op("dve", lambda e: e.tensor_copy(out=vt[:, 512:1024], in_=ps), reads=[pr], writes=[vr])
            if not halo:
                t0 = 256 + i * T + blk * 128
                gran = 2 * i + 1 + blk // 2
            else:
                t0 = blk * 128 if blk < 2 else NTOK_EXT - 256 + (blk - 2) * 128
                gran = 0 if blk < 2 else 2 * NT + 1
            S.dma("pool", v_s[t0:t0 + 128, :], vt, reads=[vr], writes=[kv_res[gran]])
        if halo:
            return
        s0, r0 = ring_load(wtm_s[2], 8192)
        s1, r1 = ring_load(wtm_s[3], 8192)
        svs = [s0.rearrange("p (kc col) -> p kc col", kc=16, col=512), s1.rearrange("p (kc col) -> p kc col", kc=16, col=512)]
        srs = [r0, r1]

        def g_proj(blk):
            pss = []
            for t in range(2):
                ps, pr = ps_get()
                for kc in range(16):
                    S.op("pe", lambda e, kc=kc: e.matmul(ps, H[:, kc, blk * 128:(blk + 1) * 128], svs[t][:, kc, :],
                                                         start=(kc == 0), stop=(kc == 15)),
                         reads=[srs[t], H_res[kc]], writes=[pr])
                pss.append((ps, pr))
            return pss

        def g_gate(blk, pss, after_gelu=None):
            for t in range(2):
                ps, pr = pss[t]
                S.op("act", lambda e: e.activation(out=GG[:, t * 512:(t + 1) * 512], in_=ps, func=AF.Gelu),
                     reads=[pr], writes=[GG_res])
            S.op("act", lambda e: e.activation(out=JUNK, in_=GG, func=AF.Square, accum_out=SSQ[:, 0:1]),
                 reads=[GG_res], writes=[JUNK_res, SSQ_res])
            S.op("act", lambda e: e.activation(out=SSQ[:, 1:2], in_=SSQ[:, 0:1], func=AF.Sqrt, bias=epsb[:, 0:1],
                                               scale=1.0 / 1024), reads=[SSQ_res, const_res], writes=[SSQ_res])
            S.op("dve", lambda e: e.reciprocal(out=SSQ[:, 2:3], in_=SSQ[:, 1:2]), reads=[SSQ_res], writes=[SSQ_res])
            S.op("dve", lambda e: e.scalar_tensor_tensor(out=GNE, in0=GG, scalar=SSQ[:, 2:3], in1=ggbE,
                                                         op0=ALU.mult, op1=ALU.mult),
                 reads=[GG_res, SSQ_res, const_res], writes=[GN_res])
            S.op("dve", lambda e: e.scalar_tensor_tensor(out=GNO, in0=GG, scalar=SSQ[:, 2:3], in1=ggbO,
                                                         op0=ALU.mult, op1=ALU.mult),
                 reads=[GG_res, SSQ_res, const_res], writes=[GN_res])
            if after_gelu is not None:
                after_gelu()
            for c4 in range(2):
                ps, pr = ps_get()
                for cc in range(4):
                    c = c4 * 4 + cc
                    S.op("pe", lambda e, c=c, cc=cc: e.matmul(ps[:, cc * 128:(cc + 1) * 128], GNE[:, c * 128:(c + 1) * 128],
                                                              wsT[:, 2 * c, :], start=True, stop=False),
                         reads=[GN_res, const_res], writes=[pr])
                    S.op("pe", lambda e, c=c, cc=cc: e.matmul(ps[:, cc * 128:(cc + 1) * 128], GNO[:, c * 128:(c + 1) * 128],
                                                              wsT[:, 2 * c + 1, :], start=False, stop=True),
                         reads=[GN_res, const_res], writes=[pr])
                S.op("dve", lambda e: e.tensor_tensor(out=GTMP, in0=ps.rearrange("p (a b) -> p a b", a=4, b=128),
                                                      in1=bsT[:, c4 * 4:(c4 + 1) * 4, :], op=ALU.add),
                     reads=[pr, const_res], writes=[GTMP_res])
                S.op("dve", lambda e: e.tensor_tensor(out=SG[:, c4 * 4:(c4 + 1) * 4, blk * 128:(blk + 1) * 128], in0=GTMP,
                                                      in1=GU[:, c4 * 4:(c4 + 1) * 4, blk * 128:(blk + 1) * 128], op=ALU.mult),
                     reads=[GTMP_res, seg[0]], writes=[seg[3], seg[4]])

        pss_cur = g_proj(0)
        for blk in range(4):
            pss_next = g_proj(blk + 1) if blk < 3 else None
            if blk == 3 and prefetch_next is not None:
                g_gate(blk, pss_cur, after_gelu=lambda: (load_x(prefetch_next, True), fm_norm(Xc, XR, 0, Hc, HR, D)))
            else:
                g_gate(blk, pss_cur)
            pss_cur = pss_next
        fm_norm([SG[:, c, :] for c in range(8)], [[seg[3]]] * 4 + [[seg[4]]] * 4, 72,
                [SGN[:, c, :] for c in range(8)], [[SGN_res]] * 8, 1024)
        S.dma("pool", sg_s[i], SGN.rearrange("p a b -> p (a b)"), reads=[SGN_res], writes=[sg_res[i]])

    def stage_b_loads(i):
        S.dma("pool", Q.rearrange("p a b -> p (a b)"), q_s[i], reads=[q_res[i]], writes=[seg[0]])
        S.dma("pool", K, k_s[:, :, i * T:i * T + 2 * T], reads=kv_res[2 * i:2 * i + 4], writes=[seg[1], seg[2]])
        S.dma("pool", V, v_s[i * T:i * T + 2 * T, :].rearrange("(j p) f -> p j f", p=128),
              reads=kv_res[2 * i:2 * i + 4], writes=[seg[3], seg[4]])
        S.dma("pool", WM[i % 2], wmask_d[:, i * 64:(i + 1) * 64], writes=[WM_res[i % 2]])
        S.dma("pool", SGN.rearrange("p a b -> p (a b)"), sg_s[i], reads=[sg_res[i]], writes=[SGN_res])

    QH_res = [Res() for _ in range(4)]

    def head_prep(i, h):
        c_ = h // 2
        po_ = (h % 2) * 64
        S.dma("pool", BT[h % 2], tbl_s[h], writes=[BT_res[h % 2]])
        vk_ = (h % 2) + 2 * ((h // 2) % 2)
        S.op("act", lambda e: e.activation(out=VH[vk_][:, :, po_:po_ + 64], in_=V[:, :, h * 64:(h + 1) * 64], func=AF.Copy),
             reads=[seg[3], seg[4]], writes=[VH_res[vk_]])
        S.op("act", lambda e: e.activation(out=QH[vk_][po_:po_ + 64, :], in_=Q[po_:po_ + 64, c_, :], func=AF.Copy),
             reads=[seg[0]], writes=[QH_res[vk_]])

    def attn_prologue(i):
        for k in range(4):
            S.op("dve", lambda e, k=k: e.memset(QH[k], 0.0), reads=[seg[5]], writes=[QH_res[k], seg[5]])
        head_prep(i, 0)

    def stage_b(i):
        wm, wmr = WM[i % 2], WM_res[i % 2]
        S.dma("pool", X.rearrange("p a b -> p (a b)"), x1_s[i], reads=[x1_res[i]], writes=X_res)
        pending = []
        n = 0
        for h in range(16):
            c = h // 2
            po = (h % 2) * 64
            qo = 64 - po
            bt, btr = BT[h % 2], BT_res[h % 2]
            vk = (h % 2) + 2 * ((h // 2) % 2)
            vh, vhr = VH[vk], VH_res[vk]
            qh, qhr = QH[vk], QH_res[vk]
            if h + 1 < 16:
                head_prep(i, h + 1)
            hs = {}
            for j in range(8):
                pss, prs = ps_get("s")
                e0 = 14 - 2 * j
                S.op("pe", lambda e: e.matmul(pss, identb, bt[:, e0 * 64:(e0 + 8) * 64], start=True, stop=False),
                     reads=[btr, const_res], writes=[prs])
                S.op("pe", lambda e: e.matmul(pss, K[:, c, j * 128:(j + 1) * 128], qh, start=False, stop=True),
                     reads=[qhr, seg[1], seg[2], seg[5]], writes=[prs])
                pt, ptr = PT[n % 5], PT_res[n % 5]
                S.op("act", lambda e: e.activation(out=pt, in_=pss, func=AF.Exp), reads=[prs], writes=[ptr])
                S.op("dve", lambda e: e.tensor_tensor(
                    out=pt.rearrange("p (a b) -> p a b", a=8, b=64), in0=pt.rearrange("p (a b) -> p a b", a=8, b=64),
                    in1=wm[:, j * 8:(j + 1) * 8].unsqueeze(2).broadcast_to([128, 8, 64]), op=ALU.mult),
                    reads=[ptr, wmr], writes=[ptr])

                def pv(c=c, po=po, qo=qo, j=j, pt=pt, ptr=ptr, hs=hs, vh=vh, vhr=vhr):
                    if j == 0:
                        hs["o"] = ps_get("o")
                    pso, pro = hs["o"]
                    S.op("pe", lambda e: e.matmul(pso, vh[:, j, :], pt, start=(j == 0), stop=(j == 7)),
                         reads=[ptr, vhr], writes=[pro])
                    if j == 7:
                        S.op("act", lambda e: e.activation(out=RSC[qo:qo + 64, :], in_=pso[qo:qo + 64, :], func=AF.Ln),
                             reads=[pro], writes=[RSC_res])
                        S.op("act", lambda e: e.activation(out=RDEN[po:po + 64, :], in_=RSC[qo:qo + 64, :], func=AF.Exp,
                                                           scale=-1.0),
                             reads=[RSC_res], writes=[RDEN_res])
                        S.op("dve", lambda e: e.tensor_tensor(out=AT[po:po + 64, c, :], in0=pso[po:po + 64, :],
                                                              in1=RDEN[po:po + 64, :], op=ALU.mult),
                             reads=[pro, RDEN_res], writes=[H_res[2 * c], H_res[2 * c + 1]])
                        if po == 64:
                            norm_stat(c, AT[:, c, :], [H_res[2 * c], H_res[2 * c + 1]])
                pending.append(pv)
                if len(pending) > 3:
                    pending.pop(0)()
                n += 1
        while pending:
            pending.pop(0)()
        fm_norm([AT[:, c, :] for c in range(8)], [[H_res[2 * c], H_res[2 * c + 1]] for c in range(8)], 64,
                [AN[:, c, :] for c in range(8)], [[seg[0]]] * 8, 1024, stats_done=True)
        for g in range(4):
            slot, sres = ring_load(wmo_s[g], 8192)
            sv = slot.rearrange("p (kc col) -> p kc col", kc=16, col=512)
            for s_ in range(4):
                m = g * 4 + s_
                ps, pr = ps_get()
                for kc in range(16):
                    rhs = AN[:, kc, :] if kc < 8 else SGN[:, kc - 8, :]
                    rres = seg[0] if kc < 8 else SGN_res
                    S.op("pe", lambda e, kc=kc, rhs=rhs: e.matmul(ps, sv[:, kc, s_ * 128:(s_ + 1) * 128], rhs,
                                                                  start=(kc == 0), stop=(kc == 15)),
                         reads=[sres, rres], writes=[pr])
                S.op("dve", lambda e: e.tensor_tensor(out=Xc[m], in0=ps, in1=Xc[m], op=ALU.add),
                     reads=[pr, X_res[m]], writes=[X_res[m]])
                norm_stat(m, Xc[m], [X_res[m]])
        fm_norm(Xc, XR, 32, Hc, HR, D, stats_done=True)
        ffn(1)
        if i + 1 < NT:
            stage_b_loads(i + 1)
        fm_norm(Xc, XR, 48, Xc, XR, D, stats_done=True)
        if i + 1 < NT:
            attn_prologue(i + 1)
        for blk in range(4):
            yt, yr = YTOK[blk % 2], YTOK_res[blk % 2]
            for cg in range(4):
                ps, pr = ps_get()
                for cc in range(4):
                    c = cg * 4 + cc
                    S.op("pe", lambda e, c=c, cc=cc: e.transpose(ps[:, cc * 128:(cc + 1) * 128],
                                                                 X[:, c, blk * 128:(blk + 1) * 128], ident),
                         reads=[X_res[c], const_res], writes=[pr])
                if cg % 2 == 0:
                    S.op("act", lambda e: e.activation(out=yt[:, cg * 512:(cg + 1) * 512], in_=ps, func=AF.Copy),
                         reads=[pr], writes=yr)
                else:
                    S.op("dve", lambda e: e.tensor_copy(out=yt[:, cg * 512:(cg + 1) * 512], in_=ps), reads=[pr], writes=yr)
            S.dma("pool", y[i * T + blk * 128: i * T + (blk + 1) * 128, :], yt, reads=yr)

    load_x(0)
    fm_norm(Xc, XR, 0, Hc, HR, D)
    for i in range(NT):
        stage_a(i, False, i + 1)
    stage_a(NT, True, None)
    S.barrier()
    for k in range(4):
        S.op("dve", lambda e, k=k: e.memset(VH[k], 1.0), writes=[VH_res[k]])
    stage_b_loads(0)
    attn_prologue(0)
    for i in range(NT):
        stage_b(i)
    S.finish()
    build_program.stats = (S.n_inst, S.n_wait)
    return nc


def host_consts():
    ident = np.eye(128, dtype=np.float32)
    bandc = np.zeros((128, 64, 128), np.float32)
    for c in range(64):
        for kc in range(64):
            m = kc - c + 15
            if 0 <= m <= 30:
                bandc[m, c, kc] = 1.0
                bandc[m, c, 64 + kc] = 1.0
    colmask = np.full((128, 64), NEG, np.float32)
    for p in range(128):
        kc = p % 64
        for c in range(64):
            cs = min(max(c - 8, 0), 48)
            if cs <= kc < cs + 16:
                colmask[p, c] = 0.0
    f = np.arange(1024)
    even = ((f // 64) % 2 == 0).astype(np.float32)
    hmask = np.concatenate([np.broadcast_to(even, (128, 1024)), np.broadcast_to(1.0 - even, (128, 1024))], axis=1)
    return ident, bandc.reshape(128, 64 * 128), colmask, np.ascontiguousarray(hmask)


def window_mask(grow0, NT, seq_of_row):
    m = np.zeros((2, NT, 8, 8), np.float32)
    for i in range(NT):
        for b in range(8):
            G = grow0 + 8 * i + b
            st, ln = seq_of_row(G)
            rs = min(max(G - 4, st), st + ln - 8)
            for j in range(8):
                for a in range(2):
                    Gk = grow0 + 8 * i + 2 * j + a - 4
                    if rs <= Gk < rs + 8:
                        m[a, i, j, b] = 1.0
    full = np.repeat(m[:, None], 64, axis=1).reshape(128, NT * 64)
    return np.ascontiguousarray(full)


def make_core_inputs(x_all, NT, grow0, seq_of_row, nrows_total, wts, consts):
    ident, bandc, colmask, hmask = consts
    t0 = grow0 * 64
    own = x_all[t0:t0 + NT * T]
    halo = np.zeros((512, D), np.float32)
    if grow0 - 4 >= 0:
        halo[0:256] = x_all[t0 - 256:t0]
    if grow0 + 8 * NT + 4 <= nrows_total:
        halo[256:512] = x_all[t0 + NT * T:t0 + NT * T + 256]
    xin = np.concatenate([own, halo], axis=0)
    d = dict(wts)
    d.update(xin=np.ascontiguousarray(xin), ident=ident, bandc=bandc, colmask=colmask, hmask=hmask,
             wmask=window_mask(grow0, NT, seq_of_row))
    return d


def prep_weights(ffn1_norm, ffn1_w_in, ffn1_w_out, mix_norm, w_in_mix, q_norm, k_norm, attn_rpb, gate_norm,
                 w_spatial, b_spatial, out_norm_a, out_norm_b, w_out_mix, ffn2_norm, ffn2_w_in, ffn2_w_out, final_norm):
    f32 = np.float32

    def fm(v, n):
        return np.asarray(v, f32).reshape(n, 128).T

    gv = np.concatenate([
        fm(ffn1_norm[0], 16), fm(mix_norm[0], 16), fm(ffn2_norm[0], 16), fm(final_norm[0], 16),
        fm(out_norm_a[0], 8), fm(out_norm_b[0], 8),
        np.tile(np.asarray(q_norm[0], f32), 2)[:, None], np.tile(np.asarray(k_norm[0], f32), 2)[:, None]], axis=1)
    ggb = np.broadcast_to(np.asarray(gate_norm[0], f32), (128, 1024))
    wsT = np.asarray(w_spatial[0], f32).transpose(2, 0, 1).reshape(128, 16 * 128)
    bs = np.asarray(b_spatial[0], f32)
    bsT = np.repeat(bs.reshape(8, 2, 128), 64, axis=1).transpose(1, 0, 2).reshape(128, 8 * 128)
    rpb = np.asarray(attn_rpb[0], f32)
    rpbT = np.zeros((128, 16, 15), f32)
    rpbT[0:31] = rpb[:, ::-1, :].transpose(2, 0, 1)
    return dict(
        w1i=np.ascontiguousarray(ffn1_w_in[0], f32), w1o=np.ascontiguousarray(ffn1_w_out[0], f32),
        wmi=np.ascontiguousarray(w_in_mix[0], f32), wmo=np.ascontiguousarray(w_out_mix[0], f32),
        w2i=np.ascontiguousarray(ffn2_w_in[0], f32), w2o=np.ascontiguousarray(ffn2_w_out[0], f32),
        gv=np.ascontiguousarray(gv, f32), ggb=np.ascontiguousarray(ggb, f32), wsT=np.ascontiguousarray(wsT),
        bsT=np.ascontiguousarray(bsT), rpbT=np.ascontiguousarray(rpbT.reshape(128, 240)))


_PROG = {}


def get_program(NT):
    if NT not in _PROG:
        _PROG[NT] = build_program(NT)
    return _PROG[NT]


def _seq_full(G):
    if G < 0 or G >= 768:
        return (-1000, 8)
    if G < 512:
        return ((G // 128) * 128, 128)
    return (512, 256)


def kernel(x_prompt, x_sample, **w):
    x_prompt = np.asarray(x_prompt, np.float32)
    x_sample = np.asarray(x_sample, np.float32)
    x_all = np.concatenate([x_prompt.reshape(-1, D), x_sample.reshape(-1, D)], axis=0)
    wts = prep_weights(**{k: np.asarray(v) for k, v in w.items()})
    consts = host_consts()
    nc = get_program(NT_FULL)
    in_maps = [make_core_inputs(x_all, NT_FULL, 96 * c, _seq_full, 768, wts, consts) for c in range(NCORES)]
    res = run_bass_kernel_spmd(nc, in_maps, core_ids=list(range(NCORES)))
    y_all = np.concatenate([np.asarray(r["y"], np.float32) for r in res.results], axis=0)
    n_p = x_prompt.shape[0] * x_prompt.shape[1]
    y_prompt = y_all[:n_p].reshape(x_prompt.shape)
    y_sample = y_all[n_p:].reshape(x_sample.shape)
    return (y_prompt, y_sample)
```

```python
import numpy as np
import concourse.bass as bass
import concourse.mybir as mybir
from concourse.bass_utils import run_bass_kernel_spmd

F32 = mybir.dt.float32
BF16 = mybir.dt.bfloat16
AF = mybir.ActivationFunctionType
ALU = mybir.AluOpType

D = 2048
C = 16
DFF = 5632
HC = 44
T = 512
NEG = -30000.0
EPS = 1e-6
NCORES = 8
NT_FULL = 12
STRICT_SAME_ENGINE = True
MASK_ENG = "dve"


class Tok:
    __slots__ = ("kind", "sem", "val", "key")

    def __init__(self, kind, sem, val, key):
        self.kind, self.sem, self.val, self.key = kind, sem, val, key


class Res:
    __slots__ = ("w", "r", "name")

    def __init__(self, name=""):
        self.w = None
        self.r = {}
        self.name = name


class Sched:
    NDMA = 24
    SEM_MAX = 4000

    def __init__(self, nc):
        self.nc = nc
        self.eng = {"pe": nc.tensor, "act": nc.scalar, "dve": nc.vector, "pool": nc.gpsimd, "sp": nc.sync}
        self.sem = {}
        self.cnt = {}
        self.semgen = {}
        self.nsem = 0
        for e in self.eng:
            self.semgen[e] = 0
            self._new_sem(e)
        self.waited = {}
        self.dma_sems = [nc.alloc_semaphore(f"dq{i}") for i in range(self.NDMA)]
        self.dma_cnt = [0] * self.NDMA
        self.dma_rr = 0
        self.dma_rr_q = {}
        self.last = {}
        self.n_inst = 0
        self.n_wait = 0

    def _new_sem(self, e):
        self.semgen[e] += 1
        self.sem[e] = self.nc.alloc_semaphore(f"s_{e}_{self.semgen[e]}")
        self.cnt[e] = 0

    def _wait(self, e, tok):
        if tok is None:
            return
        k = (e, tok.key)
        if self.waited.get(k, 0) >= tok.val:
            return
        self.eng[e].wait_ge(tok.sem, tok.val)
        self.waited[k] = tok.val
        self.n_wait += 1

    def _deps(self, e, reads, writes):
        for r in reads:
            if r.w is not None:
                self._dep1(e, r.w)
        for w in writes:
            if w.w is not None:
                self._dep1(e, w.w)
            for t in w.r.values():
                self._dep1(e, t)

    def _dep1(self, e, t):
        if t.kind == e:
            if e == "pe" or not STRICT_SAME_ENGINE:
                return
        self._wait(e, t)

    def op(self, e, fn, reads=(), writes=()):
        self._deps(e, reads, writes)
        inst = fn(self.eng[e])
        if self.cnt[e] >= self.SEM_MAX:
            self._new_sem(e)
        self.cnt[e] += 1
        inst.then_inc(self.sem[e], 1)
        tok = Tok(e, self.sem[e], self.cnt[e], (e, self.semgen[e]))
        for r in reads:
            r.r[e] = tok
        for w in writes:
            w.w = tok
            w.r = {}
        self.last[e] = tok
        self.n_inst += 1
        return tok

    def dma(self, q, out, in_, reads=(), writes=()):
        self._deps(q, reads, writes)
        lo, hi = (0, 8) if q == "sp" else (8, self.NDMA)
        i = self.dma_rr_q.get(q, lo)
        self.dma_rr_q[q] = lo + (i + 1 - lo) % (hi - lo)
        key = ("dma", i)
        if self.dma_cnt[i] > 0:
            self._wait(q, Tok("dma", self.dma_sems[i], self.dma_cnt[i] * 16, key))
        inst = self.eng[q].dma_start(out=out, in_=in_)
        self.dma_cnt[i] += 1
        inst.then_inc(self.dma_sems[i], 16)
        tok = Tok("dma", self.dma_sems[i], self.dma_cnt[i] * 16, key)
        for r in reads:
            r.r[key] = tok
        for w in writes:
            w.w = tok
            w.r = {}
        self.n_inst += 1
        return tok

    def all_tokens(self):
        toks = list(self.last.values())
        for i in range(self.NDMA):
            if self.dma_cnt[i] > 0:
                toks.append(Tok("dma", self.dma_sems[i], self.dma_cnt[i] * 16, ("dma", i)))
        return toks

    def barrier(self, engines=None):
        toks = self.all_tokens()
        for e in engines or list(self.eng):
            for t in toks:
                if t.kind == e:
                    continue
                self._wait(e, t)

    def finish(self):
        toks = self.all_tokens()
        for t in toks:
            if t.kind != "sp":
                self._wait("sp", t)


def build_program(NT):
    nc = bass.Bass("TRN2", target_bir_lowering=False)
    S = Sched(nc)
    NTA = NT + 1
    NTOK_EXT = NT * T + 512

    def dram_in(name, shape, dt=F32):
        return nc.dram_tensor(name, list(shape), dt, kind="ExternalInput").ap()

    def dram_scr(name, shape, dt):
        return nc.dram_tensor(name, list(shape), dt, kind="Internal").ap()

    xin = dram_in("xin", [NTA * T, D])
    w1i = dram_in("w1i", [D, 2 * DFF])
    w1o = dram_in("w1o", [DFF, D])
    wmi = dram_in("wmi", [D, 5120])
    wmo = dram_in("wmo", [D, D])
    w2i = dram_in("w2i", [D, 2 * DFF])
    w2o = dram_in("w2o", [DFF, D])
    gv_d = dram_in("gv", [128, 82])
    ggb_d = dram_in("ggb", [128, 1024])
    hmask_d = dram_in("hmask", [128, 2048])
    wsT_d = dram_in("wsT", [128, 16 * 128])
    bsT_d = dram_in("bsT", [128, 8 * 128])
    rpbT_d = dram_in("rpbT", [128, 240])
    bandc_d = dram_in("bandc", [128, 64 * 128])
    colmask_d = dram_in("colmask", [128, 64])
    wmask_d = dram_in("wmask", [128, NT * 64])
    ident_d = dram_in("ident", [128, 128])
    y = nc.dram_tensor("y", [NT * T, D], F32, kind="ExternalOutput").ap()

    win_s = [dram_scr(f"win_s{f}", [22, 128, 8192], BF16) for f in range(2)]
    wout_s = [dram_scr(f"wout_s{f}", [16, 128, 5632], BF16) for f in range(2)]
    wfm_s = dram_scr("wfm_s", [12, 128, 4096], BF16)
    wtm_s = dram_scr("wtm_s", [4, 128, 8192], BF16)
    wmo_s = dram_scr("wmo_s", [4, 128, 8192], BF16)
    tbl_s = dram_scr("tbl_s", [16, 128, 1408], BF16)
    x1_s = dram_scr("x1_s", [NT, 128, C * T], F32)
    q_s = dram_scr("q_s", [NT, 128, 8 * T], BF16)
    sg_s = dram_scr("sg_s", [NT, 128, 8 * T], BF16)
    k_s = dram_scr("k_s", [128, 8, NTOK_EXT], BF16)
    v_s = dram_scr("v_s", [NTOK_EXT, 1024], BF16)
    x1_res = [Res() for _ in range(NT)]
    q_res = [Res() for _ in range(NT)]
    sg_res = [Res() for _ in range(NT)]
    kv_res = [Res() for _ in range(2 * NT + 2)]

    arena_cms = {}

    def arena(name, nbytes):
        cm = nc.sbuf_tensor(name, [128, nbytes // 4], F32)
        arena_cms[name] = cm
        return cm.__enter__()

    def view(ar, off, dt, shape):
        n = 1
        for s_ in shape:
            n *= s_
        if dt == F32:
            ap = ar[:, off // 4: off // 4 + n]
        else:
            ap = ar[:, off // 4: off // 4 + n // 2].bitcast(BF16)
        if len(shape) == 2:
            ap = ap.rearrange("p (a b) -> p a b", a=shape[0], b=shape[1])
        elif len(shape) == 3:
            ap = ap.rearrange("p (a b c) -> p a b c", a=shape[0], b=shape[1], c=shape[2])
        return ap

    psum = [nc.psum_tensor(f"ps{i}", [128, 512], F32).__enter__()[:, :] for i in range(8)]
    ps_res = [Res(f"ps{i}") for i in range(8)]
    ps_rr = {"all": 0, "s": 0, "o": 0}
    PS_POOLS = {"all": list(range(8)), "s": [0, 1, 2, 3, 4], "o": [5, 6, 7]}

    def ps_get(pool="all"):
        lst = PS_POOLS[pool]
        b = lst[ps_rr[pool] % len(lst)]
        ps_rr[pool] += 1
        return psum[b], ps_res[b]

    CONST = arena("const", 19 * 1024)
    co = [0]

    def calloc(dt, shape):
        n = 1
        for s_ in shape:
            n *= s_
        nb = n * (4 if dt == F32 else 2)
        nb = (nb + 31) // 32 * 32
        ap = view(CONST, co[0], dt, shape)
        co[0] += nb
        assert co[0] <= 19 * 1024, co[0]
        return ap

    ident = calloc(F32, [128])
    onesf = calloc(F32, [128])
    blockones = calloc(F32, [128])
    onesb = calloc(BF16, [128])
    gv = calloc(F32, [82])
    ggbE = calloc(F32, [1024])
    ggbO = calloc(F32, [1024])
    wsT = calloc(BF16, [16, 128])
    bsT = calloc(F32, [8, 128])
    epsb = calloc(F32, [1])
    identb = calloc(BF16, [128])
    const_res = Res("const")

    PRE = arena("pre", 160 * 1024)
    NSTG = 3
    stg_f = [view(PRE, i * 16384, F32, [4096]) for i in range(NSTG)]
    stg_b = [view(PRE, 49152 + i * 8192, BF16, [4096]) for i in range(NSTG)]
    stg_f_res = [Res() for _ in range(NSTG)]
    stg_b_res = [Res() for _ in range(NSTG)]
    poff = 49152 + NSTG * 8192
    tmpf = view(PRE, poff, F32, [2048])
    tmpf_res = Res()
    poff += 8192
    TB = view(PRE, poff, BF16, [16 * 22 * 64])
    TB_res = Res()
    poff += 45056
    rpbT = view(PRE, poff, F32, [240])
    poff += 960
    colmask = view(PRE, poff, F32, [64])
    poff += 256
    bandc = view(PRE, poff, F32, [64 * 128])
    poff += 32768
    assert poff <= 160 * 1024, poff
    pre_res = Res()

    S.dma("sp", ident, ident_d, writes=[const_res])
    S.dma("sp", gv, gv_d, writes=[const_res])
    S.dma("sp", bsT.rearrange("p a b -> p (a b)"), bsT_d, writes=[const_res])
    S.dma("sp", rpbT, rpbT_d, writes=[pre_res])
    S.dma("sp", colmask, colmask_d, writes=[pre_res])
    S.dma("sp", bandc, bandc_d, writes=[pre_res])
    S.op("dve", lambda e: e.tensor_copy(out=identb, in_=ident), reads=[const_res], writes=[const_res])
    S.op("dve", lambda e: e.memset(onesf, 1.0), writes=[const_res])
    S.op("dve", lambda e: e.memset(onesb, 1.0), writes=[const_res])
    S.op("dve", lambda e: e.memset(epsb, EPS), writes=[const_res])
    S.op("dve", lambda e: e.memset(blockones, 0.0), writes=[const_res])
    S.op("dve", lambda e: e.memset(blockones[0:64, 0:64], 1.0), writes=[const_res])
    S.op("dve", lambda e: e.memset(blockones[64:128, 64:128], 1.0), writes=[const_res])
    S.op("dve", lambda e: e.tensor_scalar(out=gv[:, 80:81], in0=gv[:, 80:81], scalar1=0.125, scalar2=None,
                                          op0=ALU.mult), reads=[const_res], writes=[const_res])
    S.dma("sp", tmpf[:, 0:1024], ggb_d, writes=[tmpf_res])
    S.dma("sp", stg_f[0][:, 0:2048], hmask_d, writes=[stg_f_res[0]])
    S.op("dve", lambda e: e.tensor_tensor(out=ggbE, in0=tmpf[:, 0:1024], in1=stg_f[0][:, 0:1024], op=ALU.mult),
         reads=[tmpf_res, stg_f_res[0]], writes=[const_res])
    S.op("dve", lambda e: e.tensor_tensor(out=ggbO, in0=tmpf[:, 0:1024], in1=stg_f[0][:, 1024:2048], op=ALU.mult),
         reads=[tmpf_res, stg_f_res[0]], writes=[const_res])
    S.dma("sp", tmpf[:, 0:2048], wsT_d, reads=[], writes=[tmpf_res])
    S.op("dve", lambda e: e.tensor_copy(out=wsT.rearrange("p a b -> p (a b)"), in_=tmpf[:, 0:2048]),
         reads=[tmpf_res], writes=[const_res])

    S.op("dve", lambda e: e.memset(TB, NEG), writes=[TB_res])
    TB4 = TB.rearrange("p (h e c) -> p h e c", h=16, e=22, c=64)
    for c in range(64):
        ps, pr = ps_get()
        S.op("pe", lambda e, ps=ps, c=c: e.matmul(ps[:, 0:240], bandc[:, c * 128:(c + 1) * 128], rpbT[:, 0:240],
                                                   start=True, stop=True),
             reads=[pre_res], writes=[pr])
        src = ps[:, 0:240].rearrange("p (h e) -> p h e", h=16, e=15)
        eng = "act" if c % 2 == 0 else "dve"
        if eng == "act":
            S.op("act", lambda e, src=src, c=c: e.activation(out=TB4[0:64, :, 3:18, c], in_=src[0:64], func=AF.Copy),
                 reads=[pr], writes=[TB_res])
            S.op("act", lambda e, src=src, c=c: e.activation(out=TB4[64:128, :, 4:19, c], in_=src[64:128], func=AF.Copy),
                 reads=[pr], writes=[TB_res])
        else:
            S.op("dve", lambda e, src=src, c=c: e.tensor_copy(out=TB4[0:64, :, 3:18, c], in_=src[0:64]),
                 reads=[pr], writes=[TB_res])
            S.op("dve", lambda e, src=src, c=c: e.tensor_copy(out=TB4[64:128, :, 4:19, c], in_=src[64:128]),
                 reads=[pr], writes=[TB_res])
    TB3 = TB.rearrange("p (m c) -> p m c", m=352, c=64)
    S.op("dve", lambda e: e.tensor_tensor(out=TB3, in0=TB3, in1=colmask.unsqueeze(1).broadcast_to([128, 352, 64]),
                                          op=ALU.add), reads=[pre_res, TB_res], writes=[TB_res])
    S.dma("pool", tbl_s.rearrange("h p f -> p h f"), TB.rearrange("p (h f) -> p h f", h=16), reads=[TB_res])

    cv = [0]

    def convert(src, dst):
        i = cv[0] % NSTG
        eng = "act" if cv[0] % 2 == 0 else "dve"
        cv[0] += 1
        a, b = src.shape[1], src.shape[2]
        sf = stg_f[i].rearrange("p (a b) -> p a b", a=a, b=b)
        S.dma("sp", sf, src, writes=[stg_f_res[i]])
        if eng == "act":
            S.op("act", lambda e: e.activation(out=stg_b[i], in_=stg_f[i], func=AF.Copy),
                 reads=[stg_f_res[i]], writes=[stg_b_res[i]])
        else:
            S.op("dve", lambda e: e.tensor_copy(out=stg_b[i], in_=stg_f[i]),
                 reads=[stg_f_res[i]], writes=[stg_b_res[i]])
        S.dma("pool", dst, stg_b[i], reads=[stg_b_res[i]])

    def conv_ffn_in(w, ws):
        r = w.rearrange("(kc p) (gu g col) -> g gu p kc col", p=128, gu=2, col=256)
        for g in range(22):
            for gu in range(2):
                convert(r[g, gu], ws[g, :, gu * 4096:(gu + 1) * 4096])

    def conv_ffn_out(w, ws):
        r = w.rearrange("(j p) (m col) -> m p j col", p=128, col=128)
        for m in range(16):
            for hf in range(2):
                src = r[m][:, hf * 22:(hf + 1) * 22, :]
                i = cv[0] % NSTG
                eng = "act" if cv[0] % 2 == 0 else "dve"
                cv[0] += 1
                sf = stg_f[i][:, 0:2816].rearrange("p (a b) -> p a b", a=22, b=128)
                S.dma("sp", sf, src, writes=[stg_f_res[i]])
                if eng == "act":
                    S.op("act", lambda e, i=i: e.activation(out=stg_b[i][:, 0:2816], in_=stg_f[i][:, 0:2816], func=AF.Copy),
                         reads=[stg_f_res[i]], writes=[stg_b_res[i]])
                else:
                    S.op("dve", lambda e, i=i: e.tensor_copy(out=stg_b[i][:, 0:2816], in_=stg_f[i][:, 0:2816]),
                         reads=[stg_f_res[i]], writes=[stg_b_res[i]])
                S.dma("pool", ws[m, :, hf * 2816:(hf + 1) * 2816], stg_b[i][:, 0:2816], reads=[stg_b_res[i]])

    conv_ffn_in(w1i, win_s[0])
    conv_ffn_out(w1o, wout_s[0])
    FM_COL0 = {"q": 0, "k": 1024, "u": 3072}
    fm_groups = [(kd, g) for kd in ("q", "k", "u") for g in range(4)]
    for idx, (kd, g) in enumerate(fm_groups):
        c0 = FM_COL0[kd] + g * 256
        convert(wmi[:, c0:c0 + 256].rearrange("(kc p) col -> p kc col", p=128), wfm_s[idx])
    for idx in range(4):
        c0 = (2048 if idx < 2 else 4096) + (idx % 2) * 512
        r = wmi[:, c0:c0 + 512].rearrange("(kc p) col -> p kc col", p=128)
        for hf in range(2):
            convert(r[:, hf * 8:(hf + 1) * 8, :], wtm_s[idx][:, hf * 4096:(hf + 1) * 4096])
    deferred = []
    r_ = w2i.rearrange("(kc p) (gu g col) -> g gu p kc col", p=128, gu=2, col=256)
    for g in range(22):
        for gu in range(2):
            for hf in range(2):
                deferred.append((r_[g, gu][:, hf * 8:(hf + 1) * 8, :],
                                 win_s[1][g, :, gu * 4096 + hf * 2048: gu * 4096 + (hf + 1) * 2048]))
    for g in range(4):
        r = wmo[:, g * 512:(g + 1) * 512].rearrange("(kc p) col -> p kc col", p=128)
        for q4 in range(4):
            deferred.append((r[:, q4 * 4:(q4 + 1) * 4, :], wmo_s[g][:, q4 * 2048:(q4 + 1) * 2048]))
    r_ = w2o.rearrange("(j p) (m col) -> m p j col", p=128, col=128)
    for m in range(16):
        for (j0, j1) in ((0, 16), (16, 32), (32, 44)):
            deferred.append((r_[m][:, j0:j1, :], wout_s[1][m, :, j0 * 128:j1 * 128]))

    S.barrier()
    arena_cms["pre"].__exit__(None, None, None)

    XA = arena("xa", 32768)
    HA = arena("ha", 16384)
    HIDA = arena("hida", 45056)
    WRA = arena("wra", 3 * 16384)
    MISC = arena("misc", 46 * 1024)

    X = view(XA, 0, F32, [C, T])
    X_res = [Res(f"x{c}") for c in range(C)]
    H = view(HA, 0, BF16, [C, T])
    AT = view(HA, 0, F32, [8, T])
    H_res = [Res(f"h{c}") for c in range(C)]
    HR = [[r] for r in H_res]
    XR = [[r] for r in X_res]
    HID = view(HIDA, 0, BF16, [HC, T])
    seg = [Res(f"seg{i}") for i in range(6)]
    SEG_ALL = seg
    XTOK = [view(HA, 0, F32, [2048]), view(HA, 8192, F32, [2048])]
    XTOK_res = [H_res[0:8], H_res[8:16]]
    GU = view(HIDA, 0, BF16, [8, T])
    QN = view(HIDA, 8192, BF16, [8, T])
    KN = view(HIDA, 16384, BF16, [8, T])
    SG = view(HIDA, 24576, F32, [8, T])
    Q = view(HIDA, 0, BF16, [8, T])
    K = view(HIDA, 8192, BF16, [8, 2 * T])
    V = view(HIDA, 24576, BF16, [8, 1024])
    AN = view(HIDA, 0, BF16, [8, T])
    QH = [view(HIDA, 40960 + k * 1024, BF16, [T]) for k in range(4)]
    YTOK = [view(HA, 0, F32, [2048]), view(HA, 8192, F32, [2048])]
    YTOK_res = [H_res[0:8], H_res[8:16]]

    ring = [WRA[:, i * 4096:(i + 1) * 4096].bitcast(BF16) for i in range(3)]
    ring_res = [Res(f"ring{i}") for i in range(3)]
    rr = [0]

    def ring_load(src2d, n):
        i = rr[0] % 3
        rr[0] += 1
        S.dma("sp", ring[i][:, 0:n], src2d, writes=[ring_res[i]])
        return ring[i], ring_res[i]

    mo = [0]

    def malloc(dt, shape):
        n = 1
        for s_ in shape:
            n *= s_
        nb = n * (4 if dt == F32 else 2)
        nb = (nb + 31) // 32 * 32
        ap = view(MISC, mo[0], dt, shape)
        mo[0] += nb
        assert mo[0] <= 46 * 1024, mo[0]
        return ap

    SQ = [malloc(F32, [T]) for _ in range(2)]
    SQ_res = [Res() for _ in range(2)]
    ACC = malloc(F32, [T])
    ACC_res = Res()
    RSTD = malloc(F32, [T])
    RSTD_res = Res()
    STMP = [malloc(F32, [T]) for _ in range(2)]
    STMP_res = [Res() for _ in range(2)]
    PT = [malloc(BF16, [T]) for _ in range(3)]
    PT_res = [Res() for _ in range(3)]
    PT += [STMP[0][:, 0:256].bitcast(BF16), STMP[0][:, 256:512].bitcast(BF16)]
    PT_res += [STMP_res[0], STMP_res[0]]
    RSC = STMP[1]
    RSC_res = STMP_res[1]
    P2 = PT
    P2_res = PT_res
    vh_off = mo[0]
    GG = malloc(F32, [1024])
    GG_res = Res()
    GNE = malloc(BF16, [1024])
    GNO = malloc(BF16, [1024])
    GN_res = Res()
    JUNK = GNE
    JUNK_res = GN_res
    VH = [view(MISC, vh_off + k * 2048, BF16, [8, 128]) for k in range(4)]
    VH_res = [Res() for _ in range(4)]
    VTOK = [malloc(BF16, [1024]) for _ in range(2)]
    VTOK_res = [Res() for _ in range(2)]
    csf_off = mo[0]
    SGN = malloc(BF16, [8, T])
    SGN_res = Res()
    BT = [malloc(BF16, [1408]) for _ in range(2)]
    BT_res = [Res() for _ in range(2)]
    RDEN = malloc(F32, [T])
    RDEN_res = Res()
    SSQ = malloc(F32, [8])
    SSQ_res = Res()
    WM = [malloc(F32, [64]) for _ in range(2)]
    WM_res = [Res() for _ in range(2)]
    GTMP = malloc(F32, [4, 128])
    GTMP_res = Res()

    CSF = view(MISC, csf_off, F32, [2048])
    CSB = view(MISC, csf_off + 8192, BF16, [2048])
    CSB_res = [BT_res[0], BT_res[1]]
    conv = {"k": 0, "step": 0, "quota": 0}

    def conv_step():
        if conv["k"] >= len(deferred) or conv["quota"] <= 0:
            return
        src, dst = deferred[conv["k"]]
        a, b = src.shape[1], src.shape[2]
        n = a * b
        if conv["step"] == 0:
            S.dma("pool", CSF[:, 0:n].rearrange("p (a b) -> p a b", a=a, b=b), src, writes=[SGN_res])
            conv["step"] = 1
        elif conv["step"] == 1:
            if conv["k"] % 2 == 0:
                S.op("act", lambda e: e.activation(out=CSB[:, 0:n], in_=CSF[:, 0:n], func=AF.Copy),
                     reads=[SGN_res], writes=CSB_res)
            else:
                S.op("dve", lambda e: e.tensor_copy(out=CSB[:, 0:n], in_=CSF[:, 0:n]),
                     reads=[SGN_res], writes=CSB_res)
            conv["step"] = 2
        else:
            S.dma("pool", dst, CSB[:, 0:n], reads=CSB_res)
            conv["step"] = 0
            conv["k"] += 1
            conv["quota"] -= 1

    def norm_stat(c, x_ap, xr):
        if c == 0:
            S.op("act", lambda e: e.activation(out=ACC, in_=x_ap, func=AF.Square), reads=xr, writes=[ACC_res])
        else:
            sq, sr = SQ[c % 2], SQ_res[c % 2]
            S.op("act", lambda e: e.activation(out=sq, in_=x_ap, func=AF.Square), reads=xr, writes=[sr])
            S.op("dve", lambda e: e.tensor_tensor(out=ACC, in0=ACC, in1=sq, op=ALU.add),
                 reads=[sr, ACC_res], writes=[ACC_res])

    def norm_finish(xs, xres, gcol0, outs, ores, Dn):
        n = len(xs)
        ps, pr = ps_get()
        S.op("pe", lambda e: e.matmul(ps, onesf, ACC, start=True, stop=True), reads=[ACC_res, const_res], writes=[pr])
        S.op("act", lambda e: e.activation(out=RSTD, in_=ps, func=AF.Ln, bias=epsb[:, 0:1], scale=1.0 / Dn),
             reads=[pr, const_res], writes=[RSTD_res])
        S.op("act", lambda e: e.activation(out=RSTD, in_=RSTD, func=AF.Exp, scale=-0.5), reads=[RSTD_res], writes=[RSTD_res])
        for c in range(n):
            S.op("dve", lambda e, c=c: e.scalar_tensor_tensor(out=outs[c], in0=xs[c], scalar=gv[:, gcol0 + c:gcol0 + c + 1],
                                                            in1=RSTD, op0=ALU.mult, op1=ALU.mult),
                 reads=xres[c] + [RSTD_res, const_res], writes=ores[c])

    def fm_norm(xs, xres, gcol0, outs, ores, Dn, stats_done=False):
        if not stats_done:
            for c in range(len(xs)):
                norm_stat(c, xs[c], xres[c])
        norm_finish(xs, xres, gcol0, outs, ores, Dn)

    Xc = [X[:, c, :] for c in range(C)]
    Hc = [H[:, c, :] for c in range(C)]

    deferred_tail = []

    def ffn(f):
        for g in range(22):
            if g == 2:
                while deferred_tail:
                    deferred_tail.pop(0)()
            if f == 0 and g >= 3:
                conv_step()
            slot, sres = ring_load(win_s[f][g], 8192)
            sv = slot.rearrange("p (gu kc col) -> p gu kc col", gu=2, kc=16, col=256)
            for s_ in range(2):
                j = 2 * g + s_
                psg, prg = ps_get()
                psu, pru = ps_get()
                for kc in range(16):
                    S.op("pe", lambda e, kc=kc: e.matmul(psg, sv[:, 0, kc, s_ * 128:(s_ + 1) * 128], H[:, kc, :],
                                                         start=(kc == 0), stop=(kc == 15)),
                         reads=[sres, H_res[kc]], writes=[prg])
                for kc in range(16):
                    S.op("pe", lambda e, kc=kc: e.matmul(psu, sv[:, 1, kc, s_ * 128:(s_ + 1) * 128], H[:, kc, :],
                                                         start=(kc == 0), stop=(kc == 15)),
                         reads=[sres, H_res[kc]], writes=[pru])
                st, sr = STMP[j % 2], STMP_res[j % 2]
                S.op("act", lambda e: e.activation(out=st, in_=psg, func=AF.Silu), reads=[prg], writes=[sr])
                S.op("dve", lambda e: e.tensor_tensor(out=HID[:, j, :], in0=st, in1=psu, op=ALU.mult),
                     reads=[sr, pru], writes=[seg[j // 8]])
        for m in range(16):
            if f == 0:
                conv_step()
            slot, sres = ring_load(wout_s[f][m], 5632)
            sv = slot[:, 0:5632].rearrange("p (j col) -> p j col", j=44, col=128)
            ps, pr = ps_get()
            for j in range(HC):
                S.op("pe", lambda e, j=j: e.matmul(ps, sv[:, j, :], HID[:, j, :], start=(j == 0), stop=(j == HC - 1)),
                     reads=[sres, seg[j // 8]], writes=[pr])
            S.op("dve", lambda e: e.scalar_tensor_tensor(out=Xc[m], in0=ps, scalar=0.5, in1=Xc[m],
                                                         op0=ALU.mult, op1=ALU.add),
                 reads=[pr, X_res[m]], writes=[X_res[m]])
            norm_stat(m, Xc[m], [X_res[m]])

    def prefetch_x0(i):
        S.dma("pool", CSF, xin[i * T: i * T + 128, :], writes=[SGN_res])

    def load_x(i, blk0_in_csf=False):
        for blk in range(4):
            if blk == 0 and blk0_in_csf:
                xt, xr = CSF, [SGN_res]
            else:
                xt, xr = XTOK[blk % 2], XTOK_res[blk % 2]
                S.dma("pool", xt, xin[i * T + blk * 128: i * T + (blk + 1) * 128, :], writes=xr)
            for cg in range(4):
                ps, pr = ps_get()
                for cc in range(4):
                    c = cg * 4 + cc
                    S.op("pe", lambda e, c=c, cc=cc: e.transpose(ps[:, cc * 128:(cc + 1) * 128],
                                                                 xt[:, c * 128:(c + 1) * 128], ident),
                         reads=xr + [const_res], writes=[pr])
                dst = X[:, cg * 4:(cg + 1) * 4, blk * 128:(blk + 1) * 128]
                src = ps.rearrange("p (a b) -> p a b", a=4, b=128)
                if (blk * 4 + cg) % 2 == 0:
                    S.op("act", lambda e: e.activation(out=dst, in_=src, func=AF.Copy),
                         reads=[pr], writes=X_res[cg * 4:(cg + 1) * 4])
                else:
                    S.op("dve", lambda e: e.tensor_copy(out=dst, in_=src),
                         reads=[pr], writes=X_res[cg * 4:(cg + 1) * 4])

    def stage_a(i, halo, prefetch_next):
        conv["quota"] = 11
        ffn(0)
        while halo and conv["k"] < len(deferred):
            conv["quota"] = 1
            conv_step()
        if not halo:
            S.dma("pool", x1_s[i], X.rearrange("p a b -> p (a b)"), reads=X_res, writes=[x1_res[i]])
        if prefetch_next is not None:
            prefetch_x0(prefetch_next)
        fm_norm(Xc, XR, 16, Hc, HR, D, stats_done=True)
        chain = []

        def qk_chain(ps, pr, sq, sr, kd, c):
            ps2, pr2 = ps_get()
            S.op("pe", lambda e: e.matmul(ps2, blockones, sq, start=True, stop=True),
                 reads=[sr, const_res], writes=[pr2])
            S.op("act", lambda e: e.activation(out=RSTD, in_=ps2, func=AF.Ln, bias=epsb[:, 0:1], scale=1.0 / 64),
                 reads=[pr2, const_res], writes=[RSTD_res])
            S.op("act", lambda e: e.activation(out=RSTD, in_=RSTD, func=AF.Exp, scale=-0.5), reads=[RSTD_res], writes=[RSTD_res])
            dstb = QN if kd == "q" else KN
            dres = seg[1] if kd == "q" else seg[2]
            gcol = 80 if kd == "q" else 81
            S.op("dve", lambda e: e.scalar_tensor_tensor(out=dstb[:, c, :], in0=ps, scalar=gv[:, gcol:gcol + 1],
                                                         in1=RSTD, op0=ALU.mult, op1=ALU.mult),
                 reads=[pr, RSTD_res, const_res], writes=[dres])

        for idx, (kd, g) in enumerate(fm_groups):
            if halo and kd != "k":
                continue
            slot, sres = ring_load(wfm_s[idx], 4096)
            sv = slot[:, 0:4096].rearrange("p (kc col) -> p kc col", kc=16, col=256)
            for s_ in range(2):
                c = 2 * g + s_
                ps, pr = ps_get()
                for kc in range(16):
                    S.op("pe", lambda e, kc=kc: e.matmul(ps, sv[:, kc, s_ * 128:(s_ + 1) * 128], H[:, kc, :],
                                                         start=(kc == 0), stop=(kc == 15)),
                         reads=[sres, H_res[kc]], writes=[pr])
                if kd == "u":
                    while chain:
                        qk_chain(*chain.pop(0))
                    S.op("act", lambda e: e.activation(out=GU[:, c, :], in_=ps, func=AF.Gelu), reads=[pr], writes=[seg[0]])
                else:
                    sq, sr = SQ[c % 2], SQ_res[c % 2]
                    S.op("act", lambda e: e.activation(out=sq, in_=ps, func=AF.Square), reads=[pr], writes=[sr])
                    if chain:
                        qk_chain(*chain.pop(0))
                    chain.append((ps, pr, sq, sr, kd, c))
        while chain:
            qk_chain(*chain.pop(0))
        if not halo:
            S.dma("pool", q_s[i], QN.rearrange("p a b -> p (a b)"), reads=[seg[1]], writes=[q_res[i]])
            t0 = 256 + i * T
            S.dma("pool", k_s[:, :, t0:t0 + T], KN, reads=[seg[2]], writes=[kv_res[2 * i + 1], kv_res[2 * i + 2]])
        else:
            S.dma("pool", k_s[:, :, 0:256], KN[:, :, 0:256], reads=[seg[2]], writes=[kv_res[0]])
            S.dma("pool", k_s[:, :, NTOK_EXT - 256:NTOK_EXT], KN[:, :, 256:512], reads=[seg[2]],
                  writes=[kv_res[2 * NT + 1]])
        s0, r0 = ring_load(wtm_s[0], 8192)
        s1, r1 = ring_load(wtm_s[1], 8192)
        svs = [s0.rearrange("p (kc col) -> p kc col", kc=16, col=512), s1.rearrange("p (kc col) -> p kc col", kc=16, col=512)]
        srs = [r0, r1]
        for blk in range(4):
            vt, vr = VTOK[blk % 2], VTOK_res[blk % 2]
            for t in range(2):
                ps, pr = ps_get()
                for kc in range(16):
                    S.op("pe", lambda e, kc=kc: e.matmul(ps, H[:, kc, blk * 128:(blk + 1) * 128], svs[t][:, kc, :],
                                                         start=(kc == 0), stop=(kc == 15)),
                         reads=[srs[t], H_res[kc]], writes=[pr])
                if t == 0:
                    S.op("act", lambda e: e.activation(out=vt[:, 0:512], in_=ps, func=AF.Copy), reads=[pr], writes=[vr])
                else:
                    S.op("dve", lambda e: e.tensor_copy(out=vt[:, 512:1024], in_=ps), reads=[pr], writes=[vr])
            if not halo:
                t0 = 256 + i * T + blk * 128
                gran = 2 * i + 1 + blk // 2
            else:
                t0 = blk * 128 if blk < 2 else NTOK_EXT - 256 + (blk - 2) * 128
                gran = 0 if blk < 2 else 2 * NT + 1
            S.dma("pool", v_s[t0:t0 + 128, :], vt, reads=[vr], writes=[kv_res[gran]])
        if halo:
            return
        s0, r0 = ring_load(wtm_s[2], 8192)
        s1, r1 = ring_load(wtm_s[3], 8192)
        svs = [s0.rearrange("p (kc col) -> p kc col", kc=16, col=512), s1.rearrange("p (kc col) -> p kc col", kc=16, col=512)]
        srs = [r0, r1]

        def g_proj(blk):
            pss = []
            for t in range(2):
                ps, pr = ps_get()
                for kc in range(16):
                    S.op("pe", lambda e, kc=kc: e.matmul(ps, H[:, kc, blk * 128:(blk + 1) * 128], svs[t][:, kc, :],
                                                         start=(kc == 0), stop=(kc == 15)),
                         reads=[srs[t], H_res[kc]], writes=[pr])
                pss.append((ps, pr))
            return pss

        def g_gate(blk, pss, after_gelu=None):
            for t in range(2):
                ps, pr = pss[t]
                S.op("act", lambda e: e.activation(out=GG[:, t * 512:(t + 1) * 512], in_=ps, func=AF.Gelu),
                     reads=[pr], writes=[GG_res])
            S.op("act", lambda e: e.activation(out=JUNK, in_=GG, func=AF.Square, accum_out=SSQ[:, 0:1]),
                 reads=[GG_res], writes=[JUNK_res, SSQ_res])
            S.op("act", lambda e: e.activation(out=SSQ[:, 1:2], in_=SSQ[:, 0:1], func=AF.Sqrt, bias=epsb[:, 0:1],
                                               scale=1.0 / 1024), reads=[SSQ_res, const_res], writes=[SSQ_res])
            S.op("dve", lambda e: e.reciprocal(out=SSQ[:, 2:3], in_=SSQ[:, 1:2]), reads=[SSQ_res], writes=[SSQ_res])
            S.op("dve", lambda e: e.scalar_tensor_tensor(out=GNE, in0=GG, scalar=SSQ[:, 2:3], in1=ggbE,
                                                         op0=ALU.mult, op1=ALU.mult),
                 reads=[GG_res, SSQ_res, const_res], writes=[GN_res])
            S.op("dve", lambda e: e.scalar_tensor_tensor(out=GNO, in0=GG, scalar=SSQ[:, 2:3], in1=ggbO,
                                                         op0=ALU.mult, op1=ALU.mult),
                 reads=[GG_res, SSQ_res, const_res], writes=[GN_res])
            if after_gelu is not None:
                after_gelu()
            for c4 in range(2):
                ps, pr = ps_get()
                for cc in range(4):
                    c = c4 * 4 + cc
                    S.op("pe", lambda e, c=c, cc=cc: e.matmul(ps[:, cc * 128:(cc + 1) * 128], GNE[:, c * 128:(c + 1) * 128],
                                                              wsT[:, 2 * c, :], start=True, stop=False),
                         reads=[GN_res, const_res], writes=[pr])
                    S.op("pe", lambda e, c=c, cc=cc: e.matmul(ps[:, cc * 128:(cc + 1) * 128], GNO[:, c * 128:(c + 1) * 128],
                                                              wsT[:, 2 * c + 1, :], start=False, stop=True),
                         reads=[GN_res, const_res], writes=[pr])
                S.op("dve", lambda e: e.tensor_tensor(out=GTMP, in0=ps.rearrange("p (a b) -> p a b", a=4, b=128),
                                                      in1=bsT[:, c4 * 4:(c4 + 1) * 4, :], op=ALU.add),
                     reads=[pr, const_res], writes=[GTMP_res])
                S.op("dve", lambda e: e.tensor_tensor(out=SG[:, c4 * 4:(c4 + 1) * 4, blk * 128:(blk + 1) * 128], in0=GTMP,
                                                      in1=GU[:, c4 * 4:(c4 + 1) * 4, blk * 128:(blk + 1) * 128], op=ALU.mult),
                     reads=[GTMP_res, seg[0]], writes=[seg[3], seg[4]])

        pss_cur = g_proj(0)
        for blk in range(4):
            pss_next = g_proj(blk + 1) if blk < 3 else None
            if blk == 3 and prefetch_next is not None:
                g_gate(blk, pss_cur, after_gelu=lambda: (load_x(prefetch_next, True), fm_norm(Xc, XR, 0, Hc, HR, D)))
            else:
                g_gate(blk, pss_cur)
            pss_cur = pss_next
        sg_x = [SG[:, c, :] for c in range(8)]
        sg_r = [[seg[3]]] * 4 + [[seg[4]]] * 4
        for c in range(8):
            norm_stat(c, sg_x[c], sg_r[c])

        def tail(i=i):
            norm_finish(sg_x, sg_r, 72, [SGN[:, c, :] for c in range(8)], [[SGN_res]] * 8, 1024)
            S.dma("pool", sg_s[i], SGN.rearrange("p a b -> p (a b)"), reads=[SGN_res], writes=[sg_res[i]])
        deferred_tail.append(tail)

    def stage_b_loads(i):
        S.dma("pool", Q.rearrange("p a b -> p (a b)"), q_s[i], reads=[q_res[i]], writes=[seg[0]])
        S.dma("pool", K, k_s[:, :, i * T:i * T + 2 * T], reads=kv_res[2 * i:2 * i + 4], writes=[seg[1], seg[2]])
        S.dma("pool", V, v_s[i * T:i * T + 2 * T, :].rearrange("(j p) f -> p j f", p=128),
              reads=kv_res[2 * i:2 * i + 4], writes=[seg[3], seg[4]])
        S.dma("pool", WM[i % 2], wmask_d[:, i * 64:(i + 1) * 64], writes=[WM_res[i % 2]])
        S.dma("pool", SGN.rearrange("p a b -> p (a b)"), sg_s[i], reads=[sg_res[i]], writes=[SGN_res])

    QH_res = [Res() for _ in range(4)]

    def head_prep(i, h):
        c_ = h // 2
        po_ = (h % 2) * 64
        S.dma("pool", BT[h % 2], tbl_s[h], writes=[BT_res[h % 2]])
        vk_ = (h % 2) + 2 * ((h // 2) % 2)
        S.op("act", lambda e: e.activation(out=VH[vk_][:, :, po_:po_ + 64], in_=V[:, :, h * 64:(h + 1) * 64], func=AF.Copy),
             reads=[seg[3], seg[4]], writes=[VH_res[vk_]])
        S.op("act", lambda e: e.activation(out=QH[vk_][po_:po_ + 64, :], in_=Q[po_:po_ + 64, c_, :], func=AF.Copy),
             reads=[seg[0]], writes=[QH_res[vk_]])

    def attn_prologue(i):
        for k in range(4):
            S.op("dve", lambda e, k=k: e.memset(QH[k], 0.0), reads=[seg[5]], writes=[QH_res[k], seg[5]])
        head_prep(i, 0)

    def stage_b(i):
        wm, wmr = WM[i % 2], WM_res[i % 2]
        S.dma("pool", X.rearrange("p a b -> p (a b)"), x1_s[i], reads=[x1_res[i]], writes=X_res)
        pending = []
        n = 0
        for h in range(16):
            c = h // 2
            po = (h % 2) * 64
            qo = 64 - po
            bt, btr = BT[h % 2], BT_res[h % 2]
            vk = (h % 2) + 2 * ((h // 2) % 2)
            vh, vhr = VH[vk], VH_res[vk]
            qh, qhr = QH[vk], QH_res[vk]
            if h + 1 < 16:
                head_prep(i, h + 1)
            hs = {}
            for j in range(8):
                pss, prs = ps_get("s")
                e0 = 14 - 2 * j
                S.op("pe", lambda e: e.matmul(pss, identb, bt[:, e0 * 64:(e0 + 8) * 64], start=True, stop=False),
                     reads=[btr, const_res], writes=[prs])
                S.op("pe", lambda e: e.matmul(pss, K[:, c, j * 128:(j + 1) * 128], qh, start=False, stop=True),
                     reads=[qhr, seg[1], seg[2], seg[5]], writes=[prs])
                pt, ptr = PT[n % 5], PT_res[n % 5]
                S.op("act", lambda e: e.activation(out=pt, in_=pss, func=AF.Exp), reads=[prs], writes=[ptr])
                S.op("dve", lambda e: e.tensor_tensor(
                    out=pt.rearrange("p (a b) -> p a b", a=8, b=64), in0=pt.rearrange("p (a b) -> p a b", a=8, b=64),
                    in1=wm[:, j * 8:(j + 1) * 8].unsqueeze(2).broadcast_to([128, 8, 64]), op=ALU.mult),
                    reads=[ptr, wmr], writes=[ptr])

                def pv(c=c, po=po, qo=qo, j=j, pt=pt, ptr=ptr, hs=hs, vh=vh, vhr=vhr):
                    if j == 0:
                        hs["o"] = ps_get("o")
                    pso, pro = hs["o"]
                    S.op("pe", lambda e: e.matmul(pso, vh[:, j, :], pt, start=(j == 0), stop=(j == 7)),
                         reads=[ptr, vhr], writes=[pro])
                    if j == 7:
                        S.op("act", lambda e: e.activation(out=RSC[qo:qo + 64, :], in_=pso[qo:qo + 64, :], func=AF.Ln),
                             reads=[pro], writes=[RSC_res])
                        S.op("act", lambda e: e.activation(out=RDEN[po:po + 64, :], in_=RSC[qo:qo + 64, :], func=AF.Exp,
                                                           scale=-1.0),
                             reads=[RSC_res], writes=[RDEN_res])
                        S.op("dve", lambda e: e.tensor_tensor(out=AT[po:po + 64, c, :], in0=pso[po:po + 64, :],
                                                              in1=RDEN[po:po + 64, :], op=ALU.mult),
                             reads=[pro, RDEN_res], writes=[H_res[2 * c], H_res[2 * c + 1]])
                        if po == 64:
                            norm_stat(c, AT[:, c, :], [H_res[2 * c], H_res[2 * c + 1]])
                pending.append(pv)
                if len(pending) > 3:
                    pending.pop(0)()
                n += 1
        while pending:
            pending.pop(0)()
        fm_norm([AT[:, c, :] for c in range(8)], [[H_res[2 * c], H_res[2 * c + 1]] for c in range(8)], 64,
                [AN[:, c, :] for c in range(8)], [[seg[0]]] * 8, 1024, stats_done=True)
        for g in range(4):
            slot, sres = ring_load(wmo_s[g], 8192)
            sv = slot.rearrange("p (kc col) -> p kc col", kc=16, col=512)
            for s_ in range(4):
                m = g * 4 + s_
                ps, pr = ps_get()
                for kc in range(16):
                    rhs = AN[:, kc, :] if kc < 8 else SGN[:, kc - 8, :]
                    rres = seg[0] if kc < 8 else SGN_res
                    S.op("pe", lambda e, kc=kc, rhs=rhs: e.matmul(ps, sv[:, kc, s_ * 128:(s_ + 1) * 128], rhs,
                                                                  start=(kc == 0), stop=(kc == 15)),
                         reads=[sres, rres], writes=[pr])
                S.op("dve", lambda e: e.tensor_tensor(out=Xc[m], in0=ps, in1=Xc[m], op=ALU.add),
                     reads=[pr, X_res[m]], writes=[X_res[m]])
                norm_stat(m, Xc[m], [X_res[m]])
        fm_norm(Xc, XR, 32, Hc, HR, D, stats_done=True)
        ffn(1)
        if i + 1 < NT:
            stage_b_loads(i + 1)
        fm_norm(Xc, XR, 48, Xc, XR, D, stats_done=True)
        if i + 1 < NT:
            attn_prologue(i + 1)
        for blk in range(4):
            yt, yr = YTOK[blk % 2], YTOK_res[blk % 2]
            for cg in range(4):
                ps, pr = ps_get()
                for cc in range(4):
                    c = cg * 4 + cc
                    S.op("pe", lambda e, c=c, cc=cc: e.transpose(ps[:, cc * 128:(cc + 1) * 128],
                                                                 X[:, c, blk * 128:(blk + 1) * 128], ident),
                         reads=[X_res[c], const_res], writes=[pr])
                if cg % 2 == 0:
                    S.op("act", lambda e: e.activation(out=yt[:, cg * 512:(cg + 1) * 512], in_=ps, func=AF.Copy),
                         reads=[pr], writes=yr)
                else:
                    S.op("dve", lambda e: e.tensor_copy(out=yt[:, cg * 512:(cg + 1) * 512], in_=ps), reads=[pr], writes=yr)
            S.dma("pool", y[i * T + blk * 128: i * T + (blk + 1) * 128, :], yt, reads=yr)

    load_x(0)
    fm_norm(Xc, XR, 0, Hc, HR, D)
    for i in range(NT):
        stage_a(i, False, i + 1)
    stage_a(NT, True, None)
    assert not deferred_tail
    S.barrier()
    for k in range(4):
        S.op("dve", lambda e, k=k: e.memset(VH[k], 1.0), writes=[VH_res[k]])
    stage_b_loads(0)
    attn_prologue(0)
    for i in range(NT):
        stage_b(i)
    S.finish()
    build_program.stats = (S.n_inst, S.n_wait)
    return nc


def host_consts():
    ident = np.eye(128, dtype=np.float32)
    bandc = np.zeros((128, 64, 128), np.float32)
    for c in range(64):
        for kc in range(64):
            m = kc - c + 15
            if 0 <= m <= 30:
                bandc[m, c, kc] = 1.0
                bandc[m, c, 64 + kc] = 1.0
    colmask = np.full((128, 64), NEG, np.float32)
    for p in range(128):
        kc = p % 64
        for c in range(64):
            cs = min(max(c - 8, 0), 48)
            if cs <= kc < cs + 16:
                colmask[p, c] = 0.0
    f = np.arange(1024)
    even = ((f // 64) % 2 == 0).astype(np.float32)
    hmask = np.concatenate([np.broadcast_to(even, (128, 1024)), np.broadcast_to(1.0 - even, (128, 1024))], axis=1)
    return ident, bandc.reshape(128, 64 * 128), colmask, np.ascontiguousarray(hmask)


def window_mask(grow0, NT, seq_of_row):
    m = np.zeros((2, NT, 8, 8), np.float32)
    for i in range(NT):
        for b in range(8):
            G = grow0 + 8 * i + b
            st, ln = seq_of_row(G)
            rs = min(max(G - 4, st), st + ln - 8)
            for j in range(8):
                for a in range(2):
                    Gk = grow0 + 8 * i + 2 * j + a - 4
                    if rs <= Gk < rs + 8:
                        m[a, i, j, b] = 1.0
    full = np.repeat(m[:, None], 64, axis=1).reshape(128, NT * 64)
    return np.ascontiguousarray(full)


def make_core_inputs(x_all, NT, grow0, seq_of_row, nrows_total, wts, consts):
    ident, bandc, colmask, hmask = consts
    t0 = grow0 * 64
    own = x_all[t0:t0 + NT * T]
    halo = np.zeros((512, D), np.float32)
    if grow0 - 4 >= 0:
        halo[0:256] = x_all[t0 - 256:t0]
    if grow0 + 8 * NT + 4 <= nrows_total:
        halo[256:512] = x_all[t0 + NT * T:t0 + NT * T + 256]
    xin = np.concatenate([own, halo], axis=0)
    d = dict(wts)
    d.update(xin=np.ascontiguousarray(xin), ident=ident, bandc=bandc, colmask=colmask, hmask=hmask,
             wmask=window_mask(grow0, NT, seq_of_row))
    return d


def prep_weights(ffn1_norm, ffn1_w_in, ffn1_w_out, mix_norm, w_in_mix, q_norm, k_norm, attn_rpb, gate_norm,
                 w_spatial, b_spatial, out_norm_a, out_norm_b, w_out_mix, ffn2_norm, ffn2_w_in, ffn2_w_out, final_norm):
    f32 = np.float32

    def fm(v, n):
        return np.asarray(v, f32).reshape(n, 128).T

    gv = np.concatenate([
        fm(ffn1_norm[0], 16), fm(mix_norm[0], 16), fm(ffn2_norm[0], 16), fm(final_norm[0], 16),
        fm(out_norm_a[0], 8), fm(out_norm_b[0], 8),
        np.tile(np.asarray(q_norm[0], f32), 2)[:, None], np.tile(np.asarray(k_norm[0], f32), 2)[:, None]], axis=1)
    ggb = np.broadcast_to(np.asarray(gate_norm[0], f32), (128, 1024))
    wsT = np.asarray(w_spatial[0], f32).transpose(2, 0, 1).reshape(128, 16 * 128)
    bs = np.asarray(b_spatial[0], f32)
    bsT = np.repeat(bs.reshape(8, 2, 128), 64, axis=1).transpose(1, 0, 2).reshape(128, 8 * 128)
    rpb = np.asarray(attn_rpb[0], f32)
    rpbT = np.zeros((128, 16, 15), f32)
    rpbT[0:31] = rpb[:, ::-1, :].transpose(2, 0, 1)
    return dict(
        w1i=np.ascontiguousarray(ffn1_w_in[0], f32), w1o=np.ascontiguousarray(ffn1_w_out[0], f32),
        wmi=np.ascontiguousarray(w_in_mix[0], f32), wmo=np.ascontiguousarray(w_out_mix[0], f32),
        w2i=np.ascontiguousarray(ffn2_w_in[0], f32), w2o=np.ascontiguousarray(ffn2_w_out[0], f32),
        gv=np.ascontiguousarray(gv, f32), ggb=np.ascontiguousarray(ggb, f32), wsT=np.ascontiguousarray(wsT),
        bsT=np.ascontiguousarray(bsT), rpbT=np.ascontiguousarray(rpbT.reshape(128, 240)))


_PROG = {}


def get_program(NT):
    if NT not in _PROG:
        _PROG[NT] = build_program(NT)
    return _PROG[NT]


def _seq_full(G):
    if G < 0 or G >= 768:
        return (-1000, 8)
    if G < 512:
        return ((G // 128) * 128, 128)
    return (512, 256)


def kernel(x_prompt, x_sample, **w):
    x_prompt = np.asarray(x_prompt, np.float32)
    x_sample = np.asarray(x_sample, np.float32)
    x_all = np.concatenate([x_prompt.reshape(-1, D), x_sample.reshape(-1, D)], axis=0)
    wts = prep_weights(**{k: np.asarray(v) for k, v in w.items()})
    consts = host_consts()
    nc = get_program(NT_FULL)
    in_maps = [make_core_inputs(x_all, NT_FULL, 96 * c, _seq_full, 768, wts, consts) for c in range(NCORES)]
    res = run_bass_kernel_spmd(nc, in_maps, core_ids=list(range(NCORES)))
    y_all = np.concatenate([np.asarray(r["y"], np.float32) for r in res.results], axis=0)
    n_p = x_prompt.shape[0] * x_prompt.shape[1]
    y_prompt = y_all[:n_p].reshape(x_prompt.shape)
    y_sample = y_all[n_p:].reshape(x_sample.shape)
    return (y_prompt, y_sample)
```
